# Optimizing a Trainium2 kernel written in Bass

```python
import jax, jax.numpy as jnp
from jax import lax
import numpy as np

D_MODEL = 1024
BATCH = 8
SEQ = 4096
DEPTH = 2

GRID_W = 64
CTX_LEN = 256
CHUNK = 64
CONV_K = 5
GDN_HEADS = 4
GDN_DK = 128
GDN_DV = 128
RET_HEADS = 4
RET_DK = 128
RET_DV = 128
SSD_HEADS = 16
SSD_HEADDIM = 64
SSD_GROUPS = 2
SSD_STATE = 128
SSD_DINNER = SSD_HEADS * SSD_HEADDIM
D_FF = 2816
N_BRANCH = 3
N_ADA = 9
ROPE_BASE = 10000.0
GDN_QKV = GDN_HEADS * (2 * GDN_DK + GDN_DV)
SSD_XBC = SSD_DINNER + 2 * SSD_GROUPS * SSD_STATE
SPLIT_SIZES = (GDN_QKV, GDN_HEADS * GDN_DV, 2 * GDN_HEADS, 2 * GDN_HEADS,
               RET_HEADS * RET_DK, RET_HEADS * RET_DK, RET_HEADS * RET_DV, RET_HEADS * RET_DV,
               SSD_DINNER, SSD_XBC, 2 * SSD_HEADS, N_BRANCH * D_MODEL)
N_IN = sum(SPLIT_SIZES)

kernel_name = 'hybrid_gdn_retention_ssd_prefix_dit_block'


def layer_norm(x, g, b, eps=1e-5):
    xf = x.astype(jnp.float32)
    mu = jnp.mean(xf, axis=-1, keepdims=True)
    var = jnp.mean(jnp.square(xf - mu), axis=-1, keepdims=True)
    return ((xf - mu) * lax.rsqrt(var + eps) * g.astype(jnp.float32) + b.astype(jnp.float32)).astype(x.dtype)


def head_norm(x, eps=1e-5):
    xf = x.astype(jnp.float32)
    mu = jnp.mean(xf, axis=-1, keepdims=True)
    var = jnp.mean(jnp.square(xf - mu), axis=-1, keepdims=True)
    return (xf - mu) * lax.rsqrt(var + eps)


def rms_norm(x, g, eps=1e-6):
    xf = x.astype(jnp.float32)
    return xf * lax.rsqrt(jnp.mean(jnp.square(xf), axis=-1, keepdims=True) + eps) * g.astype(jnp.float32)


def l2norm(x, eps=1e-6):
    xf = x.astype(jnp.float32)
    return (xf * lax.rsqrt(jnp.sum(jnp.square(xf), axis=-1, keepdims=True) + eps)).astype(x.dtype)


def modulate(x, shift, scale):
    return x * (1 + scale) + shift


def swiglu(u, w13, w2):
    a, b = jnp.split(u @ w13, 2, axis=-1)
    return (jax.nn.silu(a) * b) @ w2


def flip(t):
    return jnp.flip(t, axis=1)


def dwconv_centred(x, w, b=None):
    pad = w.shape[0] // 2
    y = lax.conv_general_dilated(x, w[:, None, :], window_strides=(1,), padding=((pad, pad),),
                                 dimension_numbers=('NWC', 'WIO', 'NWC'),
                                 feature_group_count=x.shape[-1])
    return y if b is None else y + b


def axial_rope(seq_len):
    rows = seq_len // GRID_W
    row_id = jnp.repeat(jnp.arange(rows, dtype=jnp.float32), GRID_W)
    col_id = jnp.tile(jnp.arange(GRID_W, dtype=jnp.float32), rows)
    n_freq = RET_DK // 4
    inv_freq = ROPE_BASE ** (-jnp.arange(n_freq, dtype=jnp.float32) / n_freq)
    ang = jnp.concatenate([row_id[:, None] * inv_freq, col_id[:, None] * inv_freq], axis=-1)
    return jnp.cos(ang), jnp.sin(ang)


def apply_rope(x, cos, sin):
    half = x.shape[-1] // 2
    xf = x.astype(jnp.float32)
    x1, x2 = xf[..., :half], xf[..., half:]
    c, s = cos[None, :, None, :], sin[None, :, None, :]
    return jnp.concatenate([x1 * c - x2 * s, x2 * c + x1 * s], axis=-1).astype(x.dtype)


def chunk_delta_rule(q, k, v, beta, log_a, s0, with_output):
    b_, seq, nh, _ = q.shape
    dv = v.shape[-1]
    n = seq // CHUNK
    f32 = jnp.float32

    def to_chunks(t):
        return jnp.moveaxis(t.astype(f32).reshape(b_, n, CHUNK, nh, t.shape[-1]), (1, 3), (0, 2))

    qc, kc, vc = to_chunks(q), to_chunks(k), to_chunks(v)
    bc = to_chunks(beta[..., None])[..., 0]
    gc = jnp.cumsum(to_chunks(log_a[..., None])[..., 0], axis=-1)
    pos = jnp.arange(CHUNK)
    strict = pos[:, None] > pos[None, :]
    diff = gc[..., :, None] - gc[..., None, :]
    a_mat = (jnp.einsum('nbhik,nbhjk->nbhij', kc, kc)
             * jnp.where(strict, jnp.exp(jnp.where(strict, diff, 0.0)), 0.0) * bc[..., :, None])
    rhs = jnp.concatenate([bc[..., None] * vc, (bc * jnp.exp(gc))[..., None] * kc], axis=-1)
    sol = lax.linalg.triangular_solve(jnp.eye(CHUNK, dtype=f32) + a_mat, rhs, left_side=True, lower=True)
    w_v, w_k = sol[..., :dv], sol[..., dv:]
    k_w = kc * jnp.exp(gc[..., -1:] - gc)[..., None]
    c_dec = jnp.exp(gc[..., -1])
    if with_output:
        incl = pos[:, None] >= pos[None, :]
        p_mat = (jnp.einsum('nbhik,nbhjk->nbhij', qc, kc)
                 * jnp.where(incl, jnp.exp(jnp.where(incl, diff, 0.0)), 0.0))
        q_w = qc * jnp.exp(gc)[..., None]
        xs = (w_v, w_k, k_w, c_dec, q_w, p_mat)
    else:
        xs = (w_v, w_k, k_w, c_dec)

    def step(s, xc):
        u_c = xc[0] - jnp.einsum('bhik,bhkv->bhiv', xc[1], s)
        s_new = xc[3][..., None, None] * s + jnp.einsum('bhjk,bhjv->bhkv', xc[2], u_c)
        if with_output:
            o_c = jnp.einsum('bhik,bhkv->bhiv', xc[4], s) + jnp.einsum('bhij,bhjv->bhiv', xc[5], u_c)
            return s_new, o_c
        return s_new, None

    s_fin, o = lax.scan(step, s0.astype(f32), xs)
    if not with_output:
        return None, s_fin
    return jnp.moveaxis(o, (0, 2), (1, 3)).reshape(b_, seq, nh, dv), s_fin


def chunk_gla(q, k, v, log_a, s0, with_output):
    b_, seq, ng, kd = q.shape
    hg, dv = v.shape[3], v.shape[4]
    n = seq // CHUNK
    f32 = jnp.float32
    qc = jnp.moveaxis(q.astype(f32).reshape(b_, n, CHUNK, ng, kd), (1, 3), (0, 2))
    kc = jnp.moveaxis(k.astype(f32).reshape(b_, n, CHUNK, ng, kd), (1, 3), (0, 2))
    vc = jnp.moveaxis(v.astype(f32).reshape(b_, n, CHUNK, ng, hg, dv), (1, 3, 4), (0, 2, 3))
    gc = jnp.cumsum(jnp.moveaxis(log_a.astype(f32).reshape(b_, n, CHUNK, ng, hg), (1, 3, 4), (0, 2, 3)),
                    axis=-1)
    v_w = vc * jnp.exp(gc[..., -1:] - gc)[..., None]
    c_dec = jnp.exp(gc[..., -1])
    if with_output:
        pos = jnp.arange(CHUNK)
        incl = pos[:, None] >= pos[None, :]
        diff = gc[..., :, None] - gc[..., None, :]
        dmat = jnp.where(incl, jnp.exp(jnp.where(incl, diff, 0.0)), 0.0)
        scores = jnp.einsum('nbgik,nbgjk->nbgij', qc, kc)
        intra = jnp.einsum('nbghij,nbghjv->nbghiv', scores[:, :, :, None] * dmat, vc)
        q_dec = jnp.exp(gc)
        xs = (kc, v_w, c_dec, qc, q_dec)
    else:
        xs = (kc, v_w, c_dec)

    def step(s, xc):
        s_new = xc[2][..., None, None] * s + jnp.einsum('bgjk,bghjv->bghkv', xc[0], xc[1])
        if with_output:
            o_c = jnp.einsum('bgik,bghkv->bghiv', xc[3], s) * xc[4][..., None]
            return s_new, o_c
        return s_new, None

    s_fin, inter = lax.scan(step, s0.astype(f32), xs)
    if not with_output:
        return None, s_fin
    o = jnp.moveaxis(intra + inter, (0, 4), (1, 2)).reshape(b_, seq, ng, hg, dv)
    return o, s_fin


def zero_states(b_):
    f32 = jnp.float32
    hg = SSD_HEADS // SSD_GROUPS
    z_gdn = jnp.zeros((b_, GDN_HEADS, GDN_DK, GDN_DV), f32)
    z_ret = jnp.zeros((b_, RET_HEADS, 1, RET_DK, RET_DV), f32)
    z_ssd = jnp.zeros((b_, SSD_GROUPS, hg, SSD_STATE, SSD_HEADDIM), f32)
    return (z_gdn, z_gdn, z_ret, z_ret, z_ssd, z_ssd)


def token_mix(u, states, rope, lp, with_output):
    b_, seq, d = u.shape
    f32 = jnp.float32
    idx = [int(s) for s in np.cumsum(SPLIT_SIZES)[:-1]]
    (g_qkv, g_z, g_b, g_a, r_q, r_k, r_v, r_g, s_z, s_xbc, s_dt, br_gate) = jnp.split(
        u @ lp['w_in'], idx, axis=-1)

    qkv = jax.nn.silu(dwconv_centred(g_qkv, lp['gdn_conv_w']))
    gq, gk, gv = jnp.split(qkv, [GDN_HEADS * GDN_DK, 2 * GDN_HEADS * GDN_DK], axis=-1)
    gq = l2norm(gq.reshape(b_, seq, GDN_HEADS, GDN_DK)) * GDN_DK ** -0.5
    gk = l2norm(gk.reshape(b_, seq, GDN_HEADS, GDN_DK))
    gv = gv.reshape(b_, seq, GDN_HEADS, GDN_DV)
    g_beta = jax.nn.sigmoid(g_b.astype(f32)).reshape(b_, seq, 2, GDN_HEADS)
    g_loga = -jnp.exp(lp['gdn_a_log'].astype(f32)) * jax.nn.softplus(
        g_a.astype(f32).reshape(b_, seq, 2, GDN_HEADS) + lp['gdn_dt_bias'].astype(f32))
    oa_f, sa_f = chunk_delta_rule(gq, gk, gv, g_beta[:, :, 0], g_loga[:, :, 0], states[0], with_output)
    oa_b, sa_b = chunk_delta_rule(flip(gq), flip(gk), flip(gv), flip(g_beta[:, :, 1]), flip(g_loga[:, :, 1]),
                                  states[1], with_output)

    rq = r_q.reshape(b_, seq, RET_HEADS, RET_DK)
    rk = r_k.reshape(b_, seq, RET_HEADS, RET_DK) * RET_DK ** -0.5
    if rope is not None:
        rq = apply_rope(rq, rope[0], rope[1])
        rk = apply_rope(rk, rope[0], rope[1])
    rv = r_v.reshape(b_, seq, RET_HEADS, 1, RET_DV)
    r_loga = -jnp.exp(lp['ret_decay'].astype(f32))
    la_f = jnp.broadcast_to(r_loga[0][:, None], (b_, seq, RET_HEADS, 1))
    la_b = jnp.broadcast_to(r_loga[1][:, None], (b_, seq, RET_HEADS, 1))
    ob_f, sb_f = chunk_gla(rq, rk, rv, la_f, states[2], with_output)
    ob_b, sb_b = chunk_gla(flip(rq), flip(rk), flip(rv), la_b, states[3], with_output)

    hg = SSD_HEADS // SSD_GROUPS
    xbc = jax.nn.silu(dwconv_centred(s_xbc, lp['ssd_conv_w'], lp['ssd_conv_b']))
    sx, sb, sc = jnp.split(xbc, [SSD_DINNER, SSD_DINNER + SSD_GROUPS * SSD_STATE], axis=-1)
    sx = sx.reshape(b_, seq, SSD_GROUPS, hg, SSD_HEADDIM)
    sb = sb.reshape(b_, seq, SSD_GROUPS, SSD_STATE)
    sc = sc.reshape(b_, seq, SSD_GROUPS, SSD_STATE)
    delta = jax.nn.softplus(s_dt.astype(f32).reshape(b_, seq, 2, SSD_HEADS) + lp['ssd_dt_bias'].astype(f32))
    s_loga = (delta * -jnp.exp(lp['ssd_a_log'].astype(f32))).reshape(b_, seq, 2, SSD_GROUPS, hg)
    delta = delta.reshape(b_, seq, 2, SSD_GROUPS, hg)
    oc_f, sc_f = chunk_gla(sc, sb, sx * delta[:, :, 0, :, :, None], s_loga[:, :, 0], states[4], with_output)
    oc_b, sc_b = chunk_gla(flip(sc), flip(sb), flip(sx * delta[:, :, 1, :, :, None]), flip(s_loga[:, :, 1]),
                           states[5], with_output)

    new_states = (sa_f, sa_b, sb_f, sb_b, sc_f, sc_b)
    if not with_output:
        return None, new_states

    oa = oa_f + flip(oa_b)
    ya = rms_norm(oa, lp['gdn_norm_g']) * jax.nn.silu(g_z.astype(f32).reshape(b_, seq, GDN_HEADS, GDN_DV))
    ob = (ob_f + flip(ob_b))[:, :, :, 0]
    yb = (head_norm(ob) * lp['ret_norm_g'].astype(f32).reshape(RET_HEADS, RET_DV)
          * jax.nn.silu(r_g.astype(f32).reshape(b_, seq, RET_HEADS, RET_DV)))
    oc = oc_f + flip(oc_b) + lp['ssd_d'].astype(f32).reshape(SSD_GROUPS, hg)[:, :, None] * sx
    oc = oc.reshape(b_, seq, SSD_DINNER) * jax.nn.silu(s_z.astype(f32))
    yc = rms_norm(oc.reshape(b_, seq, SSD_GROUPS, SSD_DINNER // SSD_GROUPS),
                  lp['ssd_norm_g'].reshape(SSD_GROUPS, SSD_DINNER // SSD_GROUPS))
    ya = ya.reshape(b_, seq, -1).astype(u.dtype)
    yb = yb.reshape(b_, seq, -1).astype(u.dtype)
    yc = yc.reshape(b_, seq, -1).astype(u.dtype)
    gates = jax.nn.sigmoid(br_gate).reshape(b_, seq, N_BRANCH, d)
    merged = (gates[:, :, 0] * (ya @ lp['w_br_a']) + gates[:, :, 1] * (yb @ lp['w_br_b'])
              + gates[:, :, 2] * (yc @ lp['w_br_c']))
    return merged @ lp['w_out'], new_states


def setup_inputs(seed: int = 0) -> dict:
    key = jax.random.key(seed)
    ks = jax.random.split(key, 32)
    f32 = jnp.float32
    beta_dn = (8 * DEPTH) ** -0.25

    def nrm(k, shape, scale):
        return jax.random.normal(k, shape, f32) * scale

    def dt_bias(k, shape):
        dt = jnp.exp(jax.random.uniform(k, shape, f32, np.log(1e-3), np.log(1e-1)))
        return dt + jnp.log(-jnp.expm1(-dt))

    ret_base = jnp.asarray(np.log(-np.log(1.0 - 2.0 ** (-5.0 - np.arange(RET_HEADS)))), f32)
    return {
        'x': nrm(ks[0], (BATCH, SEQ, D_MODEL), 1.0),
        'c': nrm(ks[1], (BATCH, D_MODEL), 1.0),
        'ctx': nrm(ks[2], (BATCH, CTX_LEN, D_MODEL), 1.0),
        'c_ctx': nrm(ks[3], (D_MODEL,), 1.0),
        'ada_w': nrm(ks[4], (DEPTH, D_MODEL, N_ADA * D_MODEL), D_MODEL ** -0.5),
        'ada_b': nrm(ks[5], (DEPTH, N_ADA * D_MODEL), 0.02),
        'ln_g': 1.0 + nrm(ks[6], (DEPTH, 3, D_MODEL), 0.02),
        'ln_b': nrm(ks[7], (DEPTH, 3, D_MODEL), 0.02),
        'ffn_w13': nrm(ks[8], (DEPTH, 2, D_MODEL, 2 * D_FF), D_MODEL ** -0.5),
        'ffn_w2': nrm(ks[9], (DEPTH, 2, D_FF, D_MODEL), D_FF ** -0.5 * beta_dn),
        'mix_w_in': nrm(ks[10], (DEPTH, D_MODEL, N_IN), D_MODEL ** -0.5),
        'gdn_conv_w': nrm(ks[11], (DEPTH, CONV_K, GDN_QKV), CONV_K ** -0.5),
        'gdn_a_log': jnp.log(jax.random.uniform(ks[12], (DEPTH, 2, GDN_HEADS), f32, 1.0, 16.0)),
        'gdn_dt_bias': dt_bias(ks[13], (DEPTH, 2, GDN_HEADS)),
        'gdn_norm_g': 1.0 + nrm(ks[14], (DEPTH, GDN_DV), 0.02),
        'ret_decay': ret_base + nrm(ks[15], (DEPTH, 2, RET_HEADS), 0.05),
        'ret_norm_g': 1.0 + nrm(ks[16], (DEPTH, RET_HEADS * RET_DV), 0.02),
        'ssd_conv_w': nrm(ks[17], (DEPTH, CONV_K, SSD_XBC), CONV_K ** -0.5),
        'ssd_conv_b': nrm(ks[18], (DEPTH, SSD_XBC), 0.02),
        'ssd_a_log': jnp.log(jax.random.uniform(ks[19], (DEPTH, 2, SSD_HEADS), f32, 1.0, 16.0)),
        'ssd_dt_bias': dt_bias(ks[20], (DEPTH, 2, SSD_HEADS)),
        'ssd_d': 1.0 + nrm(ks[21], (DEPTH, SSD_HEADS), 0.02),
        'ssd_norm_g': 1.0 + nrm(ks[22], (DEPTH, SSD_DINNER), 0.02),
        'w_br_a': nrm(ks[23], (DEPTH, GDN_HEADS * GDN_DV, D_MODEL), (GDN_HEADS * GDN_DV) ** -0.5),
        'w_br_b': nrm(ks[24], (DEPTH, RET_HEADS * RET_DV, D_MODEL), (RET_HEADS * RET_DV) ** -0.5),
        'w_br_c': nrm(ks[25], (DEPTH, SSD_DINNER, D_MODEL), SSD_DINNER ** -0.5),
        'mix_w_out': nrm(ks[26], (DEPTH, D_MODEL, D_MODEL), D_MODEL ** -0.5 * beta_dn),
    }


def reference(x, c, ctx, c_ctx, ada_w, ada_b, ln_g, ln_b, ffn_w13, ffn_w2, mix_w_in,
              gdn_conv_w, gdn_a_log, gdn_dt_bias, gdn_norm_g, ret_decay, ret_norm_g,
              ssd_conv_w, ssd_conv_b, ssd_a_log, ssd_dt_bias, ssd_d, ssd_norm_g,
              w_br_a, w_br_b, w_br_c, mix_w_out):
    alpha = (2 * DEPTH) ** 0.25
    b_, seq, d = x.shape
    rope = axial_rope(seq)
    silu_c = jax.nn.silu(c)
    silu_cc = jax.nn.silu(c_ctx)
    h, hc = x, ctx
    for i in range(DEPTH):
        last = i == DEPTH - 1
        mod = (silu_c @ ada_w[i] + ada_b[i]).reshape(b_, 1, N_ADA, d)
        mod_c = (silu_cc @ ada_w[i] + ada_b[i]).reshape(N_ADA, d)
        lp = {'w_in': mix_w_in[i], 'gdn_conv_w': gdn_conv_w[i], 'gdn_a_log': gdn_a_log[i],
              'gdn_dt_bias': gdn_dt_bias[i], 'gdn_norm_g': gdn_norm_g[i], 'ret_decay': ret_decay[i],
              'ret_norm_g': ret_norm_g[i], 'ssd_conv_w': ssd_conv_w[i], 'ssd_conv_b': ssd_conv_b[i],
              'ssd_a_log': ssd_a_log[i], 'ssd_dt_bias': ssd_dt_bias[i], 'ssd_d': ssd_d[i],
              'ssd_norm_g': ssd_norm_g[i], 'w_br_a': w_br_a[i], 'w_br_b': w_br_b[i],
              'w_br_c': w_br_c[i], 'w_out': mix_w_out[i]}
        h = layer_norm(alpha * h + 0.5 * mod[:, :, 2] * swiglu(modulate(h, mod[:, :, 0], mod[:, :, 1]),
                                                             ffn_w13[i, 0], ffn_w2[i, 0]),
                       ln_g[i, 0], ln_b[i, 0])
        hc = layer_norm(alpha * hc + 0.5 * mod_c[2] * swiglu(modulate(hc, mod_c[0], mod_c[1]),
                                                            ffn_w13[i, 0], ffn_w2[i, 0]),
                        ln_g[i, 0], ln_b[i, 0])
        ctx_out, ctx_states = token_mix(modulate(hc, mod_c[3], mod_c[4]), zero_states(b_), None, lp, not last)
        lat_out, _ = token_mix(modulate(h, mod[:, :, 3], mod[:, :, 4]), ctx_states, rope, lp, True)
        h = layer_norm(alpha * h + mod[:, :, 5] * lat_out, ln_g[i, 1], ln_b[i, 1])
        h = layer_norm(alpha * h + 0.5 * mod[:, :, 8] * swiglu(modulate(h, mod[:, :, 6], mod[:, :, 7]),
                                                             ffn_w13[i, 1], ffn_w2[i, 1]),
                       ln_g[i, 2], ln_b[i, 2])
        if not last:
            hc = layer_norm(alpha * hc + mod_c[5] * ctx_out, ln_g[i, 1], ln_b[i, 1])
            hc = layer_norm(alpha * hc + 0.5 * mod_c[8] * swiglu(modulate(hc, mod_c[6], mod_c[7]),
                                                                ffn_w13[i, 1], ffn_w2[i, 1]),
                            ln_g[i, 2], ln_b[i, 2])
    return h
```

```python
import contextlib
from contextlib import ExitStack
import numpy as np
import concourse.bass as bass
import concourse.mybir as mybir
from concourse.bass_utils import run_bass_kernel_spmd

F32 = mybir.dt.float32
BF16 = mybir.dt.bfloat16
ALU = mybir.AluOpType
AF = mybir.ActivationFunctionType

D = 1024
DFF = 2816
NADA = 9
DEPTH = 2
ALPHA = float((2 * DEPTH) ** 0.25)
LN_EPS = 1e-5

ENGS = ("pe", "act", "dve", "pool", "sp")
SEM_LIMIT = 30000
import os
CUT = os.environ.get('CUT', '')


class Prog:
    def __init__(self, nc, stack):
        self.nc = nc
        self.stack = stack
        self.ops = {e: [] for e in ENGS}
        self.nsem = 0
        self.esem = {}
        self.ecnt = {}
        for e in ("pe", "act", "dve", "pool"):
            self.esem[e] = self._newsem("e_" + e)
            self.ecnt[e] = 0
        self.dsem = {}
        self.waited = {e: {} for e in ENGS}
        self.last_w = {}
        self.readers = {}
        self.n_ops = 0

    def _newsem(self, name):
        self.nsem += 1
        return self.stack.enter_context(self.nc.semaphore("s%d_%s" % (self.nsem, name)))

    def _need(self, e, dep, waits):
        sem, val, deng = dep
        if deng == "pe" and e == "pe":
            return
        w = self.waited[e].get(id(sem))
        if w is not None and w[1] >= val:
            return
        cur = waits.get(id(sem))
        if cur is None or cur[1] < val:
            waits[id(sem)] = (sem, val)

    def _deps(self, e, reads, writes):
        waits = {}
        for k in reads:
            lw = self.last_w.get(k)
            if lw is not None:
                self._need(e, lw, waits)
        for k in writes:
            lw = self.last_w.get(k)
            if lw is not None:
                self._need(e, lw, waits)
            for r in self.readers.get(k, ()):
                self._need(e, r, waits)
        for sid, (sem, val) in waits.items():
            self.waited[e][sid] = (sem, val)
        return list(waits.values())

    def _commit(self, stamp, reads, writes):
        for k in reads:
            self.readers.setdefault(k, []).append(stamp)
        for k in writes:
            self.last_w[k] = stamp
            self.readers[k] = []

    def op(self, e, meth, reads, writes, *args, **kwargs):
        def emit(eng, meth=meth, args=args, kwargs=kwargs):
            return getattr(eng, meth)(*args, **kwargs)
        waits = self._deps(e, reads, writes)
        if self.ecnt[e] >= SEM_LIMIT:
            self.esem[e] = self._newsem("e_" + e)
            self.ecnt[e] = 0
        self.ecnt[e] += 1
        sem, val = self.esem[e], self.ecnt[e]
        self.ops[e].append((waits, emit, sem, 1))
        self._commit((sem, val, e), reads, writes)
        self.n_ops += 1

    def dma(self, e, out, in_, semkey, reads=(), writes=(), nc_ok=False):
        waits = self._deps(e, reads, writes)
        ds = self.dsem.get(semkey)
        if ds is None or ds[1] * 16 >= SEM_LIMIT:
            ds = [self._newsem("d"), 0]
            self.dsem[semkey] = ds
        ds[1] += 1
        sem, val = ds[0], ds[1] * 16

        def emit(eng, out=out, in_=in_, nc_ok=nc_ok):
            if nc_ok:
                return eng.dma_start(out=out, in_=in_, allow_slow_non_contiguous=True)
            return eng.dma_start(out=out, in_=in_)
        self.ops[e].append((waits, emit, sem, 16))
        self._commit((sem, val, "dma"), reads, writes)
        self.n_ops += 1

    def barrier(self):
        stamps = []
        for e in ("pe", "act", "dve", "pool"):
            if self.ecnt[e] > 0:
                stamps.append((self.esem[e], self.ecnt[e], "x"))
        for k, ds in self.dsem.items():
            stamps.append((ds[0], ds[1] * 16, "dma"))
        for e in ENGS:
            waits = {}
            for st in stamps:
                self._need(e, st, waits)
            for sid, (sem, val) in waits.items():
                self.waited[e][sid] = (sem, val)
            if waits:
                self.ops[e].append((list(waits.values()), None, None, 0))
        self.last_w = {}
        self.readers = {}

    def emit_all(self):
        with self.nc.Block() as block:
            def run(e):
                def f(eng):
                    for waits, emit, sem, inc in self.ops[e]:
                        for (ws, wv) in waits:
                            eng.wait_ge(ws, wv)
                        if emit is not None:
                            emit(eng).then_inc(sem, inc)
                return f
            block.tensor(run("pe"))
            block.scalar(run("act"))
            block.vector(run("dve"))
            block.gpsimd(run("pool"))
            block.sync(run("sp"))


class K:
    pass


def _consts_np():
    c = np.zeros((128, 128), np.float32)
    np.fill_diagonal(c, 1.0)
    return {"ident": c}


def stage_mod(k):
    pg, nc = k.pg, k.nc
    with ExitStack() as st:
        cT = st.enter_context(nc.sbuf_tensor("cT", [128, 8, 2], F32))
        sT = st.enter_context(nc.sbuf_tensor("sT", [128, 8, 2], F32))
        aw = [st.enter_context(nc.sbuf_tensor("aw%d" % i, [128, 8, 1024], F32)) for i in range(2)]
        ab = st.enter_context(nc.sbuf_tensor("ab", [2, 1024], F32))
        mrow = [st.enter_context(nc.sbuf_tensor("mrow%d" % i, [2, 1024], F32)) for i in range(2)]
        ps = [st.enter_context(nc.psum_tensor("psm%d" % i, [2, 512], F32)) for i in range(2)]
        for s in range(2):
            pg.dma("sp", cT[:, :, s], k.cvec[s].rearrange("(k p) -> p k", p=128), semkey="cT",
                   writes=["cT"], nc_ok=True)
        pg.op("act", "activation", ["cT"], ["sT"], out=sT[:], in_=cT[:], func=AF.Silu)
        it = 0
        for l in range(DEPTH):
            for j in range(NADA):
                b = it % 2
                it += 1
                pg.dma("sp", aw[b][:], k.ada_w[l, :, j * 1024:(j + 1) * 1024].rearrange("(k p) n -> p k n", p=128),
                       semkey=("aw", b), writes=[("aw", b)])
                for s in range(2):
                    pg.dma("sp", ab[s:s + 1, :], k.ada_b[l:l + 1, j * 1024:(j + 1) * 1024], semkey="ab",
                           writes=["ab"])
                for h in range(2):
                    for kc in range(8):
                        pg.op("pe", "matmul", ["sT", ("aw", b)], [("psm", h)],
                              ps[h][:], lhsT=sT[:, kc, :], rhs=aw[b][:, kc, h * 512:(h + 1) * 512],
                              start=(kc == 0), stop=(kc == 7))
                    pg.op("dve", "tensor_tensor", [("psm", h), "ab"], [("mrow", b, h)],
                          out=mrow[b][:, h * 512:(h + 1) * 512], in0=ps[h][:], in1=ab[:, h * 512:(h + 1) * 512],
                          op=ALU.add)
                pg.dma("pool", k.modD[l, :, j * 1024:(j + 1) * 1024], mrow[b][:], semkey=("mrow", b),
                       reads=[("mrow", b, 0), ("mrow", b, 1)])
        pg.barrier()


def stage_ffn(k, li, fi, src, dst, do_ctx):
    pg, nc = k.pg, k.nc
    C, NT = k.C, k.NT
    j_shift, j_scale, j_gate = (0, 1, 2) if fi == 0 else (6, 7, 8)
    lni = 0 if fi == 0 else 2
    pfx = "f%d%d_" % (li, fi)
    with ExitStack() as st:
        def sb(name, shape, dt):
            return st.enter_context(nc.sbuf_tensor(pfx + name, shape, dt))

        def psum(name, shape):
            return st.enter_context(nc.psum_tensor(pfx + name, shape, F32))
        W13 = sb("W13", [128, 8, 2 * DFF], BF16)
        W2 = sb("W2", [128, 22, D], BF16)
        ident = sb("ident", [128, 128], F32)
        G = sb("G", [128, D], F32)
        LNG = sb("LNG", [128, D], F32)
        LNB = sb("LNB", [128, D], F32)
        sc = sb("sc", [128, 2, 8], F32)
        sh = sb("sh", [128, 2, 8], F32)
        hb = [sb("hb%d" % i, [128, D], F32) for i in range(2)]
        uT = [sb("uT%d" % i, [128, 8, 512], BF16) for i in range(2)]
        gT = sb("gT", [128, 22, 512], BF16)
        sa = [sb("sa%d" % i, [128, 512], F32) for i in range(2)]
        tmp = sb("tmp", [128, D], F32)
        lnw = ln_scratch(sb)
        psT = psum("psT", [128, 1024])
        psA = [psum("psA%d" % i, [128, 512]) for i in range(2)]
        psB = [psum("psB%d" % i, [128, 512]) for i in range(2)]
        psY = psum("psY", [128, 1024])

        pg.dma("sp", ident[:], k.ident[:, :], semkey="ident", writes=["ident"])
        for kc in range(8):
            pg.dma("pool", W13[:, kc, :], k.ffn_w13[li, fi, kc * 128:(kc + 1) * 128, :], semkey="W13",
                   writes=["W13"])
        for f0 in range(0, 22, 6):
            f1 = min(22, f0 + 6)
            pg.dma("pool", W2[:, f0:f1, :],
                   k.ffn_w2[li, fi, f0 * 128:f1 * 128, :].rearrange("(f p) d -> p f d", p=128),
                   semkey="W2", writes=["W2"])
        pg.dma("sp", LNG[:], k.ln_g[li, lni:lni + 1, :].partition_broadcast(128), semkey="LNG", writes=["LNG"])
        pg.dma("sp", LNB[:], k.ln_b[li, lni:lni + 1, :].partition_broadcast(128), semkey="LNB", writes=["LNB"])
        for s in range(2):
            pg.dma("sp", sc[:, s, :], k.modD[li, s, j_scale * 1024:(j_scale + 1) * 1024].rearrange("(k p) -> p k", p=128),
                   semkey="sc", writes=["sc"], nc_ok=True)
            pg.dma("sp", sh[:, s, :], k.modD[li, s, j_shift * 1024:(j_shift + 1) * 1024].rearrange("(k p) -> p k", p=128),
                   semkey="sh", writes=["sh"], nc_ok=True)
        pg.op("dve", "tensor_scalar_add", ["sc"], ["sc"], out=sc[:], in0=sc[:], scalar1=1.0)

        blocks = []
        if do_ctx:
            blocks.append((0, C // 128, 1))
        for t0 in range(C // 128, NT // 128, 4):
            blocks.append((t0, min(4, NT // 128 - t0), 0))

        state = {"cur_g": None, "nld": 0}

        def phA(blk, ub):
            t0, ntl, s = blk
            for tl in range(ntl):
                b = state["nld"] % 2
                state["nld"] += 1
                pg.dma("sp", hb[b][:], src(t0 + tl), semkey=("hb", b), writes=[("hb", b)])
                for kc in range(8):
                    pg.op("pe", "transpose", [("hb", b), "ident"], [("psT", kc // 4)],
                          out=psT[:, kc * 128:(kc + 1) * 128], in_=hb[b][:, kc * 128:(kc + 1) * 128],
                          identity=ident[:])
                for kc in range(8):
                    pg.op("act", "activation", [("psT", kc // 4), "sc", "sh"], [("uT", ub, tl, kc)],
                          out=uT[ub][:, kc, tl * 128:(tl + 1) * 128], in_=psT[:, kc * 128:(kc + 1) * 128],
                          func=AF.Identity, bias=sh[:, s, kc:kc + 1], scale=sc[:, s, kc:kc + 1])

        def phB(blk, ub):
            t0, ntl, s = blk
            N = ntl * 128
            for f in range(22):
                i = f % 2
                for (ps_, c0, key) in ((psA[i], f * 128, ("psA", i)), (psB[i], DFF + f * 128, ("psB", i))):
                    for kc in range(8):
                        pg.op("pe", "matmul", ["W13"] + [("uT", ub, tl, kc) for tl in range(ntl)], [key],
                              ps_[:, 0:N], lhsT=W13[:, kc, c0:c0 + 128], rhs=uT[ub][:, kc, 0:N],
                              start=(kc == 0), stop=(kc == 7))
                pg.op("act", "activation", [("psA", i)], [("sa", i)],
                      out=sa[i][:, 0:N], in_=psA[i][:, 0:N], func=AF.Silu)
                pg.op("dve", "tensor_tensor", [("sa", i), ("psB", i)], [("gT", f)],
                      out=gT[:, f, 0:N], in0=sa[i][:, 0:N], in1=psB[i][:, 0:N], op=ALU.mult)

        def phC(blk):
            t0, ntl, s = blk
            if state["cur_g"] != s:
                pg.dma("sp", G[:], k.modD[li, s:s + 1, j_gate * 1024:(j_gate + 1) * 1024].partition_broadcast(128),
                       semkey="G", writes=["G"])
                pg.op("act", "mul", ["G"], ["G"], out=G[:], in_=G[:], mul=0.5)
                state["cur_g"] = s
            for tl in range(ntl):
                b = state["nld"] % 2
                state["nld"] += 1
                hk = ("hb", b)
                pg.dma("sp", hb[b][:], src(t0 + tl), semkey=hk, writes=[hk])
                for half in range(2):
                    hs = slice(half * 512, (half + 1) * 512)
                    for f in range(22):
                        pg.op("pe", "matmul", ["W2", ("gT", f)], [("psY", half)],
                              psY[:, hs], lhsT=gT[:, f, tl * 128:(tl + 1) * 128], rhs=W2[:, f, hs],
                              start=(f == 0), stop=(f == 21))
                    pg.op("dve", "tensor_tensor", [("psY", half), "G"], [("tmp", half)],
                          out=tmp[:, hs], in0=psY[:, hs], in1=G[:, hs], op=ALU.mult)
                pg.op("dve", "scalar_tensor_tensor", [hk, ("tmp", 0), ("tmp", 1)], [hk],
                      out=hb[b][:], in0=hb[b][:], scalar=ALPHA, in1=tmp[:], op0=ALU.mult, op1=ALU.add)
                layer_norm_tile(k, hb[b], hk, lnw, LNG, LNB)
                pg.dma("pool", dst(t0 + tl), hb[b][:], semkey=("hb_st", b), reads=[hk])

        phA(blocks[0], 0)
        for n, blk in enumerate(blocks):
            phB(blk, n % 2)
            if n + 1 < len(blocks):
                phA(blocks[n + 1], (n + 1) % 2)
            phC(blk)
        pg.barrier()


def ln_scratch(sb):
    return (sb("stt", [128, 12], F32), sb("mv", [128, 2], F32), sb("rstd", [128, 1], F32))


def layer_norm_tile(k, h, hk, lnw, LNG, LNB):
    pg = k.pg
    stt, mv, rstd = lnw
    for c in range(2):
        pg.op("dve", "bn_stats", [hk], [("stt", c)], out=stt[:, c * 6:(c + 1) * 6], in_=h[:, c * 512:(c + 1) * 512])
    pg.op("dve", "bn_aggr", [("stt", 0), ("stt", 1)], ["mv"], out=mv[:], in_=stt[:])
    if CUT == "L1":
        return
    pg.op("act", "activation", ["mv"], ["rstd"], out=rstd[:], in_=mv[:, 1:2], func=AF.Sqrt, bias=LN_EPS)
    pg.op("dve", "reciprocal", ["rstd"], ["rstd"], out=rstd[:], in_=rstd[:])
    if CUT == "L2":
        return
    pg.op("dve", "tensor_scalar", [hk, "mv", "rstd"], [hk], out=h[:], in0=h[:], scalar1=mv[:, 0:1],
          scalar2=rstd[:, 0:1], op0=ALU.subtract, op1=ALU.mult)
    if CUT == "L3":
        return
    e2 = "dve"
    pg.op(e2, "tensor_tensor", [hk, "LNG"], [hk], out=h[:], in0=h[:], in1=LNG[:], op=ALU.mult)
    pg.op(e2, "tensor_tensor", [hk, "LNB"], [hk], out=h[:], in0=h[:], in1=LNB[:], op=ALU.add)


G_QKV, G_Z, G_B, G_A = 0, 1536, 2048, 2056
R_Q, R_K, R_V, R_G = 2064, 2576, 3088, 3600
S_Z, S_XBC, S_DT, BR_G = 4112, 5136, 6672, 6704
N_IN = 9776
X_LOGA = 9776
NF = 9808


def stage_mix(k, li):
    stage_scan(k, li)
    if getattr(k, "stop_scan", False):
        return
    stage_finish(k, li)


def run_lanes(tasks, nlanes):
    tasks = list(tasks)
    lanes = [None] * nlanes
    while True:
        busy = False
        for ln in range(nlanes):
            if lanes[ln] is None and tasks:
                lanes[ln] = tasks.pop(0)(ln)
            g = lanes[ln]
            if g is None:
                continue
            busy = True
            try:
                next(g)
            except StopIteration:
                lanes[ln] = None
        if not busy and not tasks:
            break


NL = 8
NLH = 4


def stage_scan(k, li):
    pg, nc = k.pg, k.nc
    C, NT = k.C, k.NT
    nch, nct = NT // 128, C // 128
    pfx = "s%d_" % li
    with ExitStack() as st:
        def sb(name, shape, dt=F32):
            return st.enter_context(nc.sbuf_tensor(pfx + name, shape, dt))

        def psum(name, shape):
            return st.enter_context(nc.psum_tensor(pfx + name, shape, F32))

        def lanes(name, shape, dt=F32, n=NL):
            return [sb("%s_l%d" % (name, i), shape, dt) for i in range(n)]
        ident = sb("ident", [128, 128])
        ones = sb("ones", [128, 128])
        msk = sb("msk", [128, 8, 128])
        rdec = sb("rdec", [128, 8])
        FA2 = [sb("FA%d" % i, [128, 12, 128]) for i in range(2)]
        FR2 = [sb("FR%d" % i, [128, 12, 128]) for i in range(2)]
        FS2 = [sb("FS%d" % i, [128, 12, 128]) for i in range(2)]
        smF2 = [sb("smF%d" % i, [128, 128]) for i in range(2)]
        sm2 = [sb("sm%d" % i, [128, 80]) for i in range(2)]
        TK2 = [sb("TK%d" % i, [128, 26, 128]) for i in range(2)]
        scG2 = [sb("scG%d" % i, [128, 2, 128]) for i in range(2)]
        dec2 = [sb("dec%d" % i, [128, 6, 24]) for i in range(2)]
        nbg2 = [sb("nbg%d" % i, [128, 8]) for i in range(2)]
        dg = lanes("dg", [128, 128])
        dT = lanes("dT", [128, 128])
        EA = lanes("EA", [128, 128], n=NLH)
        scT = lanes("scT", [128, 128])
        PT = lanes("PT", [128, 128], BF16)
        Mk = lanes("Mk", [128, 128], n=NLH)
        Pk = lanes("Pk", [128, 128], n=NLH)
        MD = lanes("MD", [128, 5, 128], n=NLH)
        PD = lanes("PD", [128, 5, 128], n=NLH)
        MO = lanes("MO", [128, 128], n=NLH)
        Yb = lanes("Yb", [128, 256], n=NLH)
        Qm = lanes("Qm", [128, 128], n=NLH)
        FT = lanes("FT", [128, 128], n=NLH)
        X = lanes("X", [128, 256], n=NLH)
        wkT = lanes("wkT", [128, 128], BF16, n=NLH)
        u = lanes("u", [128, 128], BF16, n=NLH)
        xd = lanes("xd", [128, 64], BF16)
        vw = lanes("vw", [128, 128], BF16)
        oa = lanes("oa", [128, 128])
        otok = [sb("otok%d" % i, [128, 2048]) for i in range(2)]
        g_dg = [sb("g_dg%d" % i, [128, 8, 128]) for i in range(2)]
        g_dT = [sb("g_dT%d" % i, [128, 4, 128]) for i in range(2)]
        g_PT = [sb("g_PT%d" % i, [128, 8, 128], BF16) for i in range(2)]
        g_xd = [sb("g_xd%d" % i, [128, 8, 64], BF16) for i in range(2)]
        g_vw = [sb("g_vw%d" % i, [128, 8, 64], BF16) for i in range(2)]
        g_oa = [sb("g_oa%d" % i, [128, 8, 64]) for i in range(2)]
        g_tm = [sb("g_tm%d" % i, [128, 8, 64]) for i in range(2)]
        Sg = sb("Sg", [128, 4, 128])
        Sr = sb("Sr", [128, 4, 128])
        Ss = sb("Ss", [128, 16, 64])
        Sgb = sb("Sgb", [128, 4, 128], BF16)
        Srb = sb("Srb", [128, 4, 128], BF16)
        Ssb = sb("Ssb", [128, 16, 64], BF16)
        QK2 = [sb("QK%d" % i, [128, 20, 128], BF16) for i in range(2)]
        TB2 = [sb("TB%d" % i, [128, 14, 128], BF16) for i in range(2)]
        PB = [psum("pb%d" % i, [128, 512]) for i in range(8)]

        pg.dma("sp", ident[:], k.ident[:, :], semkey="ident", writes=["ident"])
        pg.dma("sp", msk[:], k.masks.rearrange("m p f -> p m f"), semkey="msk", writes=["msk"])
        pg.op("dve", "memset", [], ["ones"], ones[:], 1.0)
        pg.dma("sp", rdec[:], k.ret_decay[li:li + 1].rearrange("o a b -> o (a b)").partition_broadcast(128),
               semkey="rdec", writes=["rdec"])
        pg.op("act", "activation", ["rdec"], ["rdec"], out=rdec[:], in_=rdec[:], func=AF.Exp)
        pg.op("dve", "tensor_scalar_mul", ["rdec"], ["rdec"], out=rdec[:], in0=rdec[:], scalar1=-1.0)
        for i in range(2):
            pg.op("dve", "memset", [], [("smF", i)], smF2[i][:], 0.0)

        def preamble(ln, ch, pb, di):
            ts = slice(ch * 128, ch * 128 + 128)
            tri = msk[:, di, :]
            FA, FR, FS, smF, sm, TK, scG, dec, nbg = (FA2[pb], FR2[pb], FS2[pb], smF2[pb], sm2[pb], TK2[pb],
                                                      scG2[pb], dec2[pb], nbg2[pb])
            kFA, kFR, kFS, ksmF, ksm, kdec = ("FA", pb), ("FR", pb), ("FS", pb), ("smF", pb), ("sm", pb), ("dec", pb)
            QK, TB, kQK = QK2[pb], TB2[pb], ("QK", pb)
            gall, Gi, Gt, eG, wj, cc = (dec[:, j, :] for j in range(6))
            nb, bg = nbg[:, 0:4], nbg[:, 4:8]
            yield pg.dma("sp", FA[:], k.PF[0:1536, ts].rearrange("(c p) t -> p c t", p=128), semkey=kFA, writes=[kFA])
            yield pg.dma("sp", FR[:], k.PF[R_Q:R_Q + 1536, ts].rearrange("(c p) t -> p c t", p=128), semkey=kFR, writes=[kFR])
            yield pg.dma("sp", FS[:], k.PF[S_XBC:S_XBC + 1536, ts].rearrange("(c p) t -> p c t", p=128), semkey=kFS, writes=[kFS])
            yield pg.dma("sp", smF[0:16, :], k.PF[G_B:G_B + 16, ts], semkey=ksmF, writes=[ksmF])
            yield pg.dma("sp", smF[16:48, :], k.PF[S_DT:S_DT + 32, ts], semkey=ksmF, writes=[ksmF])
            yield pg.dma("sp", smF[48:80, :], k.PF[X_LOGA:X_LOGA + 32, ts], semkey=ksmF, writes=[ksmF])
            yield pg.op("act", "copy", [kFA], [kQK], out=QK[:, 0:8, :], in_=FA[:, 0:8, :])
            yield pg.op("act", "copy", [kFR], [kQK], out=QK[:, 8:16, :], in_=FR[:, 0:8, :])
            yield pg.op("act", "copy", [kFS], [kQK], out=QK[:, 16:20, :], in_=FS[:, 8:12, :])
            jobs = [(smF[:], [ksmF], None)]
            for c in range(4):
                jobs.append((FA[:, 4 + c, :], [kFA], c))
            for c in range(4):
                jobs.append((FA[:, 8 + c, :], [kFA], 4 + c))
            for c in range(4):
                jobs.append((FR[:, 4 + c, :], [kFR], 8 + c))
            for c in range(4):
                jobs.append((FR[:, 8 + c, :], [kFR], 12 + c))
            for c in range(10):
                jobs.append((FS[:, c, :], [kFS], 16 + c))
            bank_i = 0
            src0, rk0, _ = jobs[0]
            bk = PB[ln]
            yield pg.op("pe", "transpose", rk0 + ["ident"], [("pb", ln)], out=bk[:, 0:128], in_=src0, identity=ident[:])
            yield pg.op("act", "copy", [("pb", ln)], [ksm], out=sm[:], in_=bk[:, 0:80])
            rest = jobs[1:]
            for g0 in range(0, len(rest), 4):
                grp = rest[g0:g0 + 4]
                bank_i += 1
                bk, bkk = PB[ln], ("pb", ln)
                for j, (src, rk, slot) in enumerate(grp):
                    yield pg.op("pe", "transpose", rk + ["ident"], [bkk], out=bk[:, j * 128:(j + 1) * 128], in_=src, identity=ident[:])
                s0 = grp[0][2]
                yield pg.op("act", "copy", [bkk], [("TK", pb, s_) for (_, _, s_) in grp],
                      out=TK[:, s0:s0 + len(grp), :], in_=bk[:, 0:len(grp) * 128].rearrange("p (c t) -> p c t", t=128))
                tb0 = {0: 0, 8: 4, 12: 8, 24: 12}.get(s0)
                if tb0 is not None:
                    yield pg.op("act", "copy", [bkk], [("TB", pb, tb0 // 4)],
                          out=TB[:, tb0:tb0 + len(grp), :], in_=bk[:, 0:len(grp) * 128].rearrange("p (c t) -> p c t", t=128))
            yield pg.op("dve", "tensor_copy", [ksm], [kdec], out=gall[:, 0:4], in_=sm[:, 8 + di * 4:12 + di * 4])
            yield pg.op("dve", "tensor_copy", ["rdec"], [kdec], out=gall[:, 4:8], in_=rdec[:, di * 4:di * 4 + 4])
            yield pg.op("dve", "tensor_copy", [ksm], [kdec], out=gall[:, 8:24], in_=sm[:, 48 + di * 16:64 + di * 16])
            bk, bkk = PB[ln], ("pb", ln)
            yield pg.op("pe", "matmul", ["msk", kdec], [bkk], bk[:, 0:24], lhsT=tri, rhs=gall, start=True, stop=True)
            yield pg.op("act", "copy", [bkk], [kdec], out=Gi, in_=bk[:, 0:24])
            bk, bkk = PB[ln], ("pb", ln)
            yield pg.op("pe", "matmul", ["ones", kdec], [bkk], bk[:, 0:24], lhsT=ones[:], rhs=gall, start=True, stop=True)
            yield pg.op("act", "copy", [bkk], [kdec], out=Gt, in_=bk[:, 0:24])
            yield pg.op("act", "activation", [kdec], [kdec], out=eG, in_=Gi, func=AF.Exp)
            yield pg.op("act", "activation", [kdec], [kdec], out=cc, in_=Gt, func=AF.Exp)
            yield pg.op("dve", "tensor_tensor", [kdec], [kdec], out=wj, in0=Gt, in1=Gi, op=ALU.subtract)
            yield pg.op("act", "activation", [kdec], [kdec], out=wj, in_=wj, func=AF.Exp)
            yield pg.op("dve", "tensor_scalar_mul", [ksm], [("nbg", pb)], out=nb, in0=sm[:, di * 4:di * 4 + 4], scalar1=-1.0)
            yield pg.op("dve", "tensor_tensor", [ksm, kdec], [("nbg", pb)], out=bg, in0=sm[:, di * 4:di * 4 + 4], in1=eG[:, 0:4],
                  op=ALU.mult)
            for g in range(2):
                bk, bkk = PB[ln], ("pb", ln)
                yield pg.op("pe", "matmul", [kQK], [bkk], bk[:, 0:128], lhsT=QK[:, 16 + g, :], rhs=QK[:, 18 + g, :], start=True, stop=True)
                yield pg.op("act", "copy", [bkk], [("scG", pb, g)], out=scG[:, g, :], in_=bk[:, 0:128])

        nout = 0
        for di in range(2):
            order = list(range(nch)) if di == 0 else (list(range(nct - 1, -1, -1)) + list(range(nch - 1, nct - 1, -1)))
            tri = msk[:, di, :]
            mD = msk[:, 2 + di, :]
            mA = msk[:, 4 + di, :]
            pg.op("dve", "memset", [], [("Sg", h_) for h_ in range(4)], Sg[:], 0.0)
            pg.op("dve", "memset", [], [("Sr", h_) for h_ in range(4)], Sr[:], 0.0)
            pg.op("dve", "memset", [], [("Ss", h_) for h_ in range(16)], Ss[:], 0.0)
            pg.op("dve", "memset", [], [("Sgb", h_) for h_ in range(4)], Sgb[:], 0.0)
            pg.op("dve", "memset", [], [("Srb", h_) for h_ in range(4)], Srb[:], 0.0)
            pg.op("dve", "memset", [], [("Ssb", h_) for h_ in range(16)], Ssb[:], 0.0)
            for _ in preamble(7, order[0], nout % 2, di):
                pass
            for oi, ch in enumerate(order):
                t0 = ch * 128
                ts = slice(t0, t0 + 128)
                pb = nout % 2
                FA, FR, FS, smF, sm, TK, scG, dec, nbg = (FA2[pb], FR2[pb], FS2[pb], smF2[pb], sm2[pb], TK2[pb],
                                                          scG2[pb], dec2[pb], nbg2[pb])
                kFA, kFR, kFS, ksmF, ksm, kdec = ("FA", pb), ("FR", pb), ("FS", pb), ("smF", pb), ("sm", pb), ("dec", pb)
                QK, TB, kQK = QK2[pb], TB2[pb], ("QK", pb)
                gall, Gi, Gt, eG, wj, cc = (dec[:, j, :] for j in range(6))
                nb, bg = nbg[:, 0:4], nbg[:, 4:8]
                ot = otok[pb]
                nout += 1
                kdecs = [kdec, ("nbg", pb), ksm]

                def head(ln, hc, qT, kT, ktok, vtok, vkeys, dv, S, skey, ocol, sc_src, qkeys, qTb, kTb, ktokb, vtokb, Sb, sbkey,
                         gdn_h=None, xd_src=None, QK=QK, TB=TB, kQK=kQK, ot=ot, pb=pb, Gi=Gi, eG=eG, wj=wj, cc=cc, nb=nb, bg=bg, sm=sm, kdecs=kdecs):
                    b0, b1 = PB[ln][:, 0:256], PB[ln][:, 256:512]
                    k0 = k1 = ("pb", ln)
                    L = lambda n: (n, ln)
                    if xd_src is not None:
                        xsrc, xkeys, dcol = xd_src
                        yield pg.op("dve", "tensor_scalar_mul", xkeys + kdecs, [L("xd")], out=xd[ln][:], in0=xsrc, scalar1=sm[:, dcol:dcol + 1])
                    yield pg.op("dve", "tensor_scalar_mul", ["ident"] + kdecs, [L("dg")], out=dg[ln][:], in0=ident[:], scalar1=Gi[:, hc:hc + 1])
                    yield pg.op("pe", "matmul", ["ones", L("dg")], [k0], b0[:, 0:128], lhsT=ones[:], rhs=dg[ln][:], start=True, stop=True)
                    yield pg.op("dve", "scalar_tensor_tensor", [k0, "msk"] + kdecs, [L("dT")], out=dT[ln][:], in0=b0[:, 0:128],
                                scalar=Gi[:, hc:hc + 1], in1=mD, op0=ALU.subtract, op1=ALU.add)
                    if gdn_h is not None:
                        h = gdn_h
                        yield pg.op("dve", "scalar_tensor_tensor", [k0, "msk"] + kdecs, [L("EA")], out=EA[ln][:], in0=b0[:, 0:128],
                                    scalar=Gi[:, hc:hc + 1], in1=mA, op0=ALU.subtract, op1=ALU.subtract)
                    yield pg.op("act", "activation", [L("dT")], [L("dT")], out=dT[ln][:], in_=dT[ln][:], func=AF.Exp)
                    if gdn_h is not None:
                        yield pg.op("act", "activation", [L("EA")], [L("EA")], out=EA[ln][:], in_=EA[ln][:], func=AF.Exp, scale=-1.0)
                        yield pg.op("pe", "matmul", qkeys, [k1], b1[:, 0:128], lhsT=kT, rhs=kT, start=True, stop=True)
                        yield pg.op("dve", "scalar_tensor_tensor", [k1, L("EA")] + kdecs, [L("Pk")], out=Pk[ln][:], in0=b1[:, 0:128],
                                    scalar=nb[:, h:h + 1], in1=EA[ln][:], op0=ALU.mult, op1=ALU.mult)
                        yield pg.op("pe", "transpose", [L("Pk"), "ident"], [k0], out=b0[:, 0:128], in_=Pk[ln][:], identity=ident[:])
                        yield pg.op("act", "copy", [k0], [L("Mk")], out=Mk[ln][:], in_=b0[:, 0:128])
                        yield pg.op("dve", "tensor_scalar_mul", vkeys + kdecs, [L("X")], out=X[ln][:, 0:128], in0=vtok,
                                    scalar1=sm[:, di * 4 + h:di * 4 + h + 1])
                        yield pg.op("dve", "tensor_scalar_mul", vkeys + kdecs, [L("X")], out=X[ln][:, 128:256], in0=ktok, scalar1=bg[:, h:h + 1])
                        yield pg.op("dve", "tensor_tensor", [L("Pk"), "msk"], [L("PD0")], out=PD[ln][:, 0, :], in0=Pk[ln][:], in1=msk[:, 6, :], op=ALU.mult)
                        yield pg.op("dve", "tensor_tensor", [L("Mk"), "msk"], [L("MD0")], out=MD[ln][:, 0, :], in0=Mk[ln][:], in1=msk[:, 6, :], op=ALU.mult)
                        yield pg.op("dve", "tensor_tensor", [L("Pk"), "msk"], [L("MO")], out=MO[ln][:], in0=Pk[ln][:], in1=msk[:, 7, :], op=ALU.mult)
                        yield pg.op("dve", "tensor_tensor", [L("MD0"), "ident"], [L("Q")], out=Qm[ln][:], in0=MD[ln][:, 0, :], in1=ident[:], op=ALU.add)
                        for lev in range(4):
                            md, pd = L("MD%d" % lev), L("PD%d" % lev)
                            yield pg.op("pe", "matmul", [md, pd], [k1], b1[:, 0:128], lhsT=MD[ln][:, lev, :], rhs=PD[ln][:, lev, :],
                                        start=True, stop=True)
                            if lev < 3:
                                yield pg.op("pe", "matmul", [md, pd], [k0], b0[:, 0:128], lhsT=PD[ln][:, lev, :], rhs=MD[ln][:, lev, :],
                                            start=True, stop=True)
                            yield pg.op("act", "copy", [k1], [L("PD%d" % (lev + 1))], out=PD[ln][:, lev + 1, :], in_=b1[:, 0:128])
                            if lev < 3:
                                yield pg.op("act", "copy", [k0], [L("MD%d" % (lev + 1))], out=MD[ln][:, lev + 1, :], in_=b0[:, 0:128])
                            yield pg.op("pe", "matmul", [L("PD%d" % (lev + 1)), L("Q")], [k1], b1[:, 0:128], lhsT=PD[ln][:, lev + 1, :], rhs=Qm[ln][:],
                                        start=True, stop=True)
                            yield pg.op("dve", "tensor_tensor", [L("Q"), k1], [L("Q")], out=Qm[ln][:], in0=Qm[ln][:], in1=b1[:, 0:128], op=ALU.add)
                        yield pg.op("pe", "matmul", [L("MO"), L("Q")], [k0], b0[:, 0:128], lhsT=MO[ln][:], rhs=Qm[ln][:], start=True, stop=True)
                        yield pg.op("act", "copy", [k0], [L("FT")], out=FT[ln][:], in_=b0[:, 0:128])
                        yield pg.op("pe", "matmul", [L("Q"), L("X")], [k1], b1[:, 0:256], lhsT=Qm[ln][:], rhs=X[ln][:], start=True, stop=True)
                        yield pg.op("act", "copy", [k1], [L("Yb")], out=Yb[ln][:], in_=b1[:, 0:256])
                        for it in range(3):
                            src, srck = (Yb[ln], L("Yb")) if it == 0 else (X[ln], L("X"))
                            bb, kk_ = (b0, k0) if it % 2 == 0 else (b1, k1)
                            yield pg.op("pe", "matmul", [L("FT"), srck], [kk_], bb[:, 0:256], lhsT=FT[ln][:], rhs=src[:], start=True, stop=True)
                            yield pg.op("dve", "tensor_tensor", [L("Yb"), kk_], [L("X")], out=X[ln][:], in0=Yb[ln][:], in1=bb[:, 0:256], op=ALU.add)
                        yield pg.op("pe", "transpose", [L("X"), "ident"], [k0], out=b0[:, 0:128], in_=X[ln][:, 128:256], identity=ident[:])
                        yield pg.op("act", "copy", [k0], [L("wkT")], out=wkT[ln][:], in_=b0[:, 0:128])
                        yield pg.op("pe", "matmul", [L("wkT"), sbkey], [k1], b1[:, 0:128], lhsT=wkT[ln][:], rhs=Sb, start=True, stop=True)
                        yield pg.op("dve", "tensor_tensor", [L("X"), k1], [L("u")], out=u[ln][:], in0=X[ln][:, 0:128], in1=b1[:, 0:128], op=ALU.subtract)
                        vtok, vkeys = u[ln][:], [L("u")]
                        vtokb = u[ln][:]
                    if xd_src is not None:
                        vtok, vkeys = xd[ln][:], [L("xd")] + vkeys
                        vtokb = xd[ln][:]
                    if sc_src is None:
                        yield pg.op("pe", "matmul", [kQK], [k1], b1[:, 0:128], lhsT=kTb, rhs=qTb, start=True, stop=True)
                        yield pg.op("act", "copy", [k1], [L("scT")], out=scT[ln][:], in_=b1[:, 0:128])
                        sc_ap, sc_k = scT[ln][:], L("scT")
                    else:
                        sc_ap, sc_k = sc_src
                    yield pg.op("dve", "tensor_tensor", [sc_k, L("dT")], [L("PT")], out=PT[ln][:], in0=sc_ap, in1=dT[ln][:], op=ALU.mult)
                    yield pg.op("pe", "matmul", [L("PT"), ("TB", pb, 2)] + vkeys, [k0], b0[:, 0:dv], lhsT=PT[ln][:], rhs=vtokb, start=True, stop=True)
                    yield pg.op("pe", "matmul", [kQK, sbkey], [k1], b1[:, 0:dv], lhsT=qTb, rhs=Sb, start=True, stop=True)
                    yield pg.op("act", "copy", [k0], [L("oa")], out=oa[ln][:, 0:dv], in_=b0[:, 0:dv])
                    yield pg.op("dve", "scalar_tensor_tensor", [k1, L("oa")] + kdecs, [("otok", pb, hc)], out=ot[:, ocol:ocol + dv], in0=b1[:, 0:dv],
                                scalar=eG[:, hc:hc + 1], in1=oa[ln][:, 0:dv], op0=ALU.mult, op1=ALU.add)
                    yield pg.op("dve", "tensor_scalar_mul", vkeys + kdecs, [L("vw")], out=vw[ln][:, 0:dv], in0=vtok, scalar1=wj[:, hc:hc + 1])
                    yield pg.op("pe", "matmul", [L("vw"), ("TB", pb, 0), ("TB", pb, 1), ("TB", pb, 3)], [k0], b0[:, 0:dv], lhsT=ktokb, rhs=vw[ln][:, 0:dv],
                                start=True, stop=True)
                    yield pg.op("dve", "scalar_tensor_tensor", [skey, k0] + kdecs, [skey], out=S, in0=S, scalar=cc[:, hc:hc + 1],
                                in1=b0[:, 0:dv], op0=ALU.mult, op1=ALU.add)
                    yield pg.op("act", "copy", [skey], [sbkey], out=Sb, in_=S)

                tasks = []
                for h in range(4):
                    tasks.append(lambda ln, h=h: head(ln, h, FA[:, h, :], FA[:, 4 + h, :], TK[:, h, :], TK[:, 4 + h, :],
                                                      [("TK", pb, h), ("TK", pb, 4 + h)], 128, Sg[:, h, :], ("Sg", h), h * 128,
                                                      None, [kFA], QK[:, h, :], QK[:, 4 + h, :], TB[:, h, :], None, Sgb[:, h, :], ("Sgb", h),
                                                      gdn_h=h))
                for h in range(4):
                    tasks.append(lambda ln, h=h: head(ln, 4 + h, FR[:, h, :], FR[:, 4 + h, :], TK[:, 8 + h, :], TK[:, 12 + h, :],
                                                      [("TK", pb, 8 + h), ("TK", pb, 12 + h)], 128, Sr[:, h, :], ("Sr", h),
                                                      512 + h * 128, None, [kFR], QK[:, 8 + h, :], QK[:, 12 + h, :], TB[:, 4 + h, :],
                                                      TB[:, 8 + h, :], Srb[:, h, :], ("Srb", h)))
                def ssd_group(ln, g, ot=ot, pb=pb, Gi=Gi, eG=eG, wj=wj, cc=cc, sm=sm, kdecs=kdecs, TK=TK, QK=QK, TB=TB, scG=scG, kQK=kQK):
                    bank, bkey = PB[ln], ("pb", ln)
                    h0 = 8 * g
                    hc0 = 8 + h0
                    G = lambda *n: tuple(n) + ("g", g)
                    xk = [("TK", pb, 16 + 4 * g + j) for j in range(4)]
                    x8 = TK[:, 16 + 4 * g:20 + 4 * g, :].rearrange("p c (two d) -> p (c two) d", two=2)
                    dl = sm[:, 16 + di * 16 + h0:16 + di * 16 + h0 + 8].unsqueeze(2).broadcast_to([128, 8, 64])
                    yield pg.op("dve", "tensor_tensor", xk + kdecs, [G("xd")], out=g_xd[g][:], in0=x8, in1=dl, op=ALU.mult)
                    yield pg.op("dve", "tensor_tensor", ["ident"] + kdecs, [G("dg")], out=g_dg[g][:],
                                in0=ident[:].unsqueeze(1).broadcast_to([128, 8, 128]),
                                in1=Gi[:, hc0:hc0 + 8].unsqueeze(2).broadcast_to([128, 8, 128]), op=ALU.mult)
                    for hf in range(2):
                        b3 = bank[:, 0:512].rearrange("p (h t) -> p h t", h=4)
                        gi4 = Gi[:, hc0 + 4 * hf:hc0 + 4 * hf + 4].unsqueeze(2).broadcast_to([128, 4, 128])
                        yield pg.op("pe", "matmul", ["ones", G("dg")], [bkey], bank[:, 0:512], lhsT=ones[:],
                                    rhs=g_dg[g][:, 4 * hf:4 * hf + 4, :].rearrange("p h t -> p (h t)"), start=True, stop=True)
                        yield pg.op("dve", "tensor_tensor", [bkey] + kdecs, [G("dT")], out=g_dT[g][:], in0=b3, in1=gi4, op=ALU.subtract)
                        yield pg.op("dve", "tensor_tensor", [G("dT"), "msk"], [G("dT")], out=g_dT[g][:], in0=g_dT[g][:],
                                    in1=mD.unsqueeze(1).broadcast_to([128, 4, 128]), op=ALU.add)
                        yield pg.op("act", "activation", [G("dT")], [G("dT")], out=g_dT[g][:], in_=g_dT[g][:], func=AF.Exp)
                        yield pg.op("dve", "tensor_tensor", [G("dT"), ("scG", pb, g)], [G("PT", hf)], out=g_PT[g][:, 4 * hf:4 * hf + 4, :],
                                    in0=g_dT[g][:], in1=scG[:, g, :].unsqueeze(1).broadcast_to([128, 4, 128]), op=ALU.mult)
                    for j in range(8):
                        yield pg.op("pe", "matmul", [G("PT", j // 4), G("xd")], [bkey], bank[:, j * 64:(j + 1) * 64], lhsT=g_PT[g][:, j, :],
                                    rhs=g_xd[g][:, j, :], start=True, stop=True)
                    b8 = bank[:, 0:512].rearrange("p (h d) -> p h d", h=8)
                    skeys = [("Ss", h0 + j) for j in range(8)]
                    sbkeys = [("Ssb", h0 + j) for j in range(8)]
                    yield pg.op("act", "copy", [bkey], [G("oa")], out=g_oa[g][:], in_=b8)
                    yield pg.op("pe", "matmul", [kQK] + sbkeys, [bkey], bank[:, 0:512], lhsT=QK[:, 18 + g, :],
                                rhs=Ssb[:, h0:h0 + 8, :].rearrange("p h d -> p (h d)"), start=True, stop=True)
                    yield pg.op("dve", "tensor_tensor", [bkey] + kdecs, [G("tm")], out=g_tm[g][:], in0=b8,
                                in1=eG[:, hc0:hc0 + 8].unsqueeze(2).broadcast_to([128, 8, 64]), op=ALU.mult)
                    yield pg.op("dve", "tensor_tensor", [G("tm"), G("oa")], [("otok", pb, hc0 + j) for j in range(8)],
                                out=ot[:, 1024 + h0 * 64:1024 + (h0 + 8) * 64].rearrange("p (h d) -> p h d", h=8), in0=g_tm[g][:], in1=g_oa[g][:],
                                op=ALU.add)
                    yield pg.op("dve", "tensor_tensor", [G("xd")] + kdecs, [G("vw")], out=g_vw[g][:], in0=g_xd[g][:],
                                in1=wj[:, hc0:hc0 + 8].unsqueeze(2).broadcast_to([128, 8, 64]), op=ALU.mult)
                    yield pg.op("pe", "matmul", [G("vw"), ("TB", pb, 3)], [bkey], bank[:, 0:512], lhsT=TB[:, 12 + g, :],
                                rhs=g_vw[g][:].rearrange("p h d -> p (h d)"), start=True, stop=True)
                    yield pg.op("dve", "tensor_tensor", skeys + kdecs, skeys, out=Ss[:, h0:h0 + 8, :], in0=Ss[:, h0:h0 + 8, :],
                                in1=cc[:, hc0:hc0 + 8].unsqueeze(2).broadcast_to([128, 8, 64]), op=ALU.mult)
                    yield pg.op("dve", "tensor_tensor", skeys + [bkey], skeys, out=Ss[:, h0:h0 + 8, :], in0=Ss[:, h0:h0 + 8, :], in1=b8, op=ALU.add)
                    yield pg.op("act", "copy", skeys, sbkeys, out=Ssb[:, h0:h0 + 8, :], in_=Ss[:, h0:h0 + 8, :])

                ssd_tasks = [lambda ln, g=g: ssd_group(ln, g) for g in range(2)]
                tasks = tasks[0:4] + ssd_tasks + tasks[4:]
                if oi + 1 < len(order):
                    tasks.insert(6, lambda ln, nch_=order[oi + 1], pb_=1 - pb: preamble(ln, nch_, pb_, di))
                run_lanes(tasks, NL)
                pg.dma("sp", k.OF[di, ts, :], ot[:], semkey=("ot_st", pb), reads=[("otok", pb, hc_) for hc_ in range(24)])
        pg.barrier()


def stage_finish(k, li):
    pg, nc = k.pg, k.nc
    C, NT = k.C, k.NT
    nch, nct = NT // 128, C // 128
    last = li == DEPTH - 1
    pfx = "m%d_" % li
    AXX = mybir.AxisListType.X
    with ExitStack() as st:
        def sb(name, shape, dt=F32):
            return st.enter_context(nc.sbuf_tensor(pfx + name, shape, dt))

        def psum(name, shape):
            return st.enter_context(nc.psum_tensor(pfx + name, shape, F32))
        ident = sb("ident", [128, 128])
        Wa = sb("Wa", [128, 4, D], BF16)
        Wb = sb("Wb", [128, 4, D], BF16)
        Wc = sb("Wc", [128, 8, D], BF16)
        Wo = sb("Wo", [128, 8, D], BF16)
        NG = sb("NG", [128, 2048])
        Dx = sb("Dx", [128, 16])
        G5 = sb("G5", [128, D])
        LNG = sb("LNG", [128, D])
        LNB = sb("LNB", [128, D])
        of = sb("of", [128, 2048])
        ob = sb("ob", [128, 2048])
        FZ = sb("FZ", [128, 16, 128])
        FX = sb("FX", [128, 8, 128])
        FB = sb("FB", [128, 24, 128])
        ZG = sb("ZG", [128, 2048])
        XS = sb("XS", [128, 1024])
        BG = sb("BG", [128, 3072])
        sq = sb("sq", [128, 2048])
        st1 = sb("st1", [128, 16])
        st2 = sb("st2", [128, 16])
        YT = sb("YT", [128, 16, 128], BF16)
        mg = sb("mg", [128, D])
        MT = sb("MT", [128, 8, 128], BF16)
        hb = sb("hb", [128, D])
        tmp = sb("tmp", [128, D])
        lnw = ln_scratch(sb)
        psT = psum("psT", [128, 1024])
        psB = psum("psB", [128, 1024])
        psO = psum("psO", [128, 1024])

        pg.dma("sp", ident[:], k.ident[:, :], semkey="ident", writes=["ident"])
        pg.dma("pool", Wa[:], k.w_br_a[li].rearrange("(c p) n -> p c n", p=128), semkey="Wa", writes=["Wa"])
        pg.dma("pool", Wb[:], k.w_br_b[li].rearrange("(c p) n -> p c n", p=128), semkey="Wb", writes=["Wb"])
        pg.dma("pool", Wc[:], k.w_br_c[li].rearrange("(c p) n -> p c n", p=128), semkey="Wc", writes=["Wc"])
        pg.dma("pool", Wo[:], k.mix_w_out[li].rearrange("(c p) n -> p c n", p=128), semkey="Wo", writes=["Wo"])
        for h in range(4):
            pg.dma("sp", NG[:, h * 128:(h + 1) * 128], k.gdn_norm_g[li:li + 1, :].partition_broadcast(128), semkey="NG", writes=["NG"])
        pg.dma("sp", NG[:, 512:1024], k.ret_norm_g[li:li + 1, :].partition_broadcast(128), semkey="NG", writes=["NG"])
        pg.dma("sp", NG[:, 1024:2048], k.ssd_norm_g[li:li + 1, :].partition_broadcast(128), semkey="NG", writes=["NG"])
        pg.dma("sp", Dx[:], k.ssd_d[li:li + 1, :].partition_broadcast(128), semkey="Dx", writes=["Dx"])
        pg.dma("sp", LNG[:], k.ln_g[li, 1:2, :].partition_broadcast(128), semkey="LNG", writes=["LNG"])
        pg.dma("sp", LNB[:], k.ln_b[li, 1:2, :].partition_broadcast(128), semkey="LNB", writes=["LNB"])

        def TT(dst, src, n, rk, wk, func=None, dt_out=None):
            for c0 in range(0, n, 8):
                m = min(8, n - c0)
                for c in range(m):
                    pg.op("pe", "transpose", rk + ["ident"], [("psT", c // 4)], out=psT[:, c * 128:(c + 1) * 128],
                          in_=src(c0 + c), identity=ident[:])
                for half in range((m + 3) // 4):
                    w = min(4, m - half * 4) * 128
                    if func is None:
                        pg.op("act", "copy", [("psT", half)], wk, out=dst(c0 + half * 4, w), in_=psT[:, half * 512:half * 512 + w])
                    else:
                        pg.op("act", "activation", [("psT", half)], wk, out=dst(c0 + half * 4, w),
                              in_=psT[:, half * 512:half * 512 + w], func=func)

        cur = None
        for t in range(nch):
            is_ctx = t < nct
            if is_ctx and last:
                continue
            s = 1 if is_ctx else 0
            ts = slice(t * 128, (t + 1) * 128)
            if cur != s:
                pg.dma("sp", G5[:], k.modD[li, s:s + 1, 5 * 1024:6 * 1024].partition_broadcast(128), semkey="G5", writes=["G5"])
                cur = s
            pg.dma("sp", of[:], k.OF[0, ts, :], semkey="of", writes=["of"])
            pg.dma("sp", ob[:], k.OF[1, ts, :], semkey="ob", writes=["ob"])
            pg.dma("sp", FZ[:, 0:4, :], k.PF[G_Z:G_Z + 512, ts].rearrange("(c p) t -> p c t", p=128), semkey="FZ", writes=["FZ"])
            pg.dma("sp", FZ[:, 4:8, :], k.PF[R_G:R_G + 512, ts].rearrange("(c p) t -> p c t", p=128), semkey="FZ", writes=["FZ"])
            pg.dma("sp", FZ[:, 8:16, :], k.PF[S_Z:S_Z + 1024, ts].rearrange("(c p) t -> p c t", p=128), semkey="FZ", writes=["FZ"])
            pg.dma("sp", FX[:], k.PF[S_XBC:S_XBC + 1024, ts].rearrange("(c p) t -> p c t", p=128), semkey="FX", writes=["FX"])
            pg.dma("sp", FB[:], k.PF[BR_G:BR_G + 3072, ts].rearrange("(c p) t -> p c t", p=128), semkey="FB", writes=["FB"])
            pg.dma("sp", hb[:], k.hD[ts, :], semkey="hb", writes=["hb"])
            TT(lambda c, w: ZG[:, c * 128:c * 128 + w], lambda c: FZ[:, c, :], 16, ["FZ"], ["ZG"], func=AF.Silu)
            TT(lambda c, w: XS[:, c * 128:c * 128 + w], lambda c: FX[:, c, :], 8, ["FX"], ["XS"])
            TT(lambda c, w: BG[:, c * 128:c * 128 + w], lambda c: FB[:, c, :], 24, ["FB"], ["BG"])
            o = of
            pg.op("dve", "tensor_tensor", ["of", "ob"], ["of"], out=o[:], in0=of[:], in1=ob[:], op=ALU.add)
            pg.op("dve", "tensor_tensor", ["XS", "Dx"], ["XS"], out=XS[:].rearrange("p (h d) -> p h d", h=16),
                  in0=XS[:].rearrange("p (h d) -> p h d", h=16), in1=Dx[:, 0:16].unsqueeze(2).broadcast_to([128, 16, 64]), op=ALU.mult)
            pg.op("dve", "tensor_tensor", ["of", "XS"], ["of"], out=o[:, 1024:2048], in0=o[:, 1024:2048], in1=XS[:], op=ALU.add)
            pg.op("dve", "tensor_tensor", ["of", "ZG"], ["of"], out=o[:, 1024:2048], in0=o[:, 1024:2048], in1=ZG[:, 1024:2048], op=ALU.mult)
            pg.op("dve", "tensor_tensor", ["of"], ["sq"], out=sq[:], in0=o[:], in1=o[:], op=ALU.mult)
            pg.op("dve", "tensor_reduce", ["sq"], ["st2"], out=st2[:, 0:8], in_=sq[:, 0:1024].rearrange("p (h d) -> p h d", h=8), axis=AXX, op=ALU.add)
            pg.op("dve", "tensor_reduce", ["sq"], ["st2"], out=st2[:, 8:10], in_=sq[:, 1024:2048].rearrange("p (h d) -> p h d", h=2), axis=AXX, op=ALU.add)
            pg.op("dve", "tensor_reduce", ["of"], ["st1"], out=st1[:, 0:4], in_=o[:, 512:1024].rearrange("p (h d) -> p h d", h=4), axis=AXX, op=ALU.add)
            pg.op("act", "activation", ["st2"], ["st2"], out=st2[:, 0:4], in_=st2[:, 0:4], func=AF.Sqrt, scale=1.0 / 128, bias=1e-6)
            pg.op("act", "activation", ["st2"], ["st2"], out=st2[:, 8:10], in_=st2[:, 8:10], func=AF.Sqrt, scale=1.0 / 512, bias=1e-6)
            pg.op("dve", "tensor_scalar_mul", ["st1"], ["st1"], out=st1[:, 0:4], in0=st1[:, 0:4], scalar1=1.0 / 128)
            pg.op("dve", "tensor_tensor", ["st1"], ["st1"], out=st1[:, 4:8], in0=st1[:, 0:4], in1=st1[:, 0:4], op=ALU.mult)
            pg.op("dve", "scalar_tensor_tensor", ["st2", "st1"], ["st2"], out=st2[:, 4:8], in0=st2[:, 4:8], scalar=1.0 / 128,
                  in1=st1[:, 4:8], op0=ALU.mult, op1=ALU.subtract)
            pg.op("act", "activation", ["st2"], ["st2"], out=st2[:, 4:8], in_=st2[:, 4:8], func=AF.Sqrt, bias=1e-5)
            pg.op("dve", "reciprocal", ["st2"], ["st2"], out=st2[:, 0:10], in_=st2[:, 0:10])
            pg.op("dve", "tensor_tensor", ["of", "st1"], ["of"], out=o[:, 512:1024].rearrange("p (h d) -> p h d", h=4),
                  in0=o[:, 512:1024].rearrange("p (h d) -> p h d", h=4), in1=st1[:, 0:4].unsqueeze(2).broadcast_to([128, 4, 128]), op=ALU.subtract)
            pg.op("dve", "tensor_tensor", ["of", "st2"], ["of"], out=o[:, 0:1024].rearrange("p (h d) -> p h d", h=8),
                  in0=o[:, 0:1024].rearrange("p (h d) -> p h d", h=8), in1=st2[:, 0:8].unsqueeze(2).broadcast_to([128, 8, 128]), op=ALU.mult)
            pg.op("dve", "tensor_tensor", ["of", "st2"], ["of"], out=o[:, 1024:2048].rearrange("p (h d) -> p h d", h=2),
                  in0=o[:, 1024:2048].rearrange("p (h d) -> p h d", h=2), in1=st2[:, 8:10].unsqueeze(2).broadcast_to([128, 2, 512]), op=ALU.mult)
            pg.op("dve", "tensor_tensor", ["of", "NG"], ["of"], out=o[:], in0=o[:], in1=NG[:], op=ALU.mult)
            pg.op("dve", "tensor_tensor", ["of", "ZG"], ["of"], out=o[:, 0:1024], in0=o[:, 0:1024], in1=ZG[:, 0:1024], op=ALU.mult)
            TT(lambda c, w: YT[:, c:c + w // 128, :], lambda c: o[:, c * 128:(c + 1) * 128], 16, ["of"], ["YT"])
            for bi, (W, c0, ncn, wk) in enumerate(((Wa, 0, 4, "Wa"), (Wb, 4, 4, "Wb"), (Wc, 8, 8, "Wc"))):
                for half in range(2):
                    hs = slice(half * 512, (half + 1) * 512)
                    for c in range(ncn):
                        pg.op("pe", "matmul", ["YT", wk], [("psB", half)], psB[:, hs], lhsT=YT[:, c0 + c, :], rhs=W[:, c, hs],
                              start=(c == 0), stop=(c == ncn - 1))
                    gs = slice(bi * 1024 + half * 512, bi * 1024 + (half + 1) * 512)
                    if bi == 0:
                        pg.op("dve", "tensor_tensor", [("psB", half), "BG"], [("mg", half)], out=mg[:, hs], in0=psB[:, hs], in1=BG[:, gs], op=ALU.mult)
                    else:
                        pg.op("dve", "tensor_tensor", [("psB", half), "BG"], [("tmp", half)], out=tmp[:, hs], in0=psB[:, hs], in1=BG[:, gs], op=ALU.mult)
                        pg.op("dve", "tensor_tensor", [("mg", half), ("tmp", half)], [("mg", half)], out=mg[:, hs], in0=mg[:, hs], in1=tmp[:, hs], op=ALU.add)
            TT(lambda c, w: MT[:, c:c + w // 128, :], lambda c: mg[:, c * 128:(c + 1) * 128], 8, [("mg", 0), ("mg", 1)], ["MT"])
            for half in range(2):
                hs = slice(half * 512, (half + 1) * 512)
                for c in range(8):
                    pg.op("pe", "matmul", ["MT", "Wo"], [("psO", half)], psO[:, hs], lhsT=MT[:, c, :], rhs=Wo[:, c, hs],
                          start=(c == 0), stop=(c == 7))
                pg.op("dve", "tensor_tensor", [("psO", half), "G5"], [("tmp", half)], out=tmp[:, hs], in0=psO[:, hs], in1=G5[:, hs], op=ALU.mult)
            pg.op("dve", "scalar_tensor_tensor", ["hb", ("tmp", 0), ("tmp", 1)], ["hb"], out=hb[:], in0=hb[:], scalar=ALPHA, in1=tmp[:],
                  op0=ALU.mult, op1=ALU.add)
            layer_norm_tile(k, hb, "hb", lnw, LNG, LNB)
            pg.dma("pool", k.hD[ts, :], hb[:], semkey="hb_st", reads=["hb"])
        pg.barrier()


def stage_proj(k, li):
    pg, nc = k.pg, k.nc
    C, NT, L = k.C, k.NT, k.L
    pfx = "p%d_" % li
    with ExitStack() as st:
        def sb(name, shape, dt):
            return st.enter_context(nc.sbuf_tensor(pfx + name, shape, dt))

        def psum(name, shape):
            return st.enter_context(nc.psum_tensor(pfx + name, shape, F32))
        ident = sb("ident", [128, 128], F32)
        ones = sb("ones", [128, 128], F32)
        rott = sb("rott", [128, 128], F32)
        sc = sb("sc", [128, 2, 8], F32)
        sh = sb("sh", [128, 2, 8], F32)
        hb = [sb("hb%d" % i, [128, D], F32) for i in range(2)]
        uT = sb("uT", [128, 8, NT], BF16)
        Wsup = [[sb("Wsup%d_%d" % (i, j), [128, 8, 512], BF16) for j in range(2)] for i in range(2)]
        sup_cur = [(-1, 1), (-1, 1)]
        SEGS = [(0, 1536), (1536, 512), (2048, 8), (2056, 8), (2064, 1024), (3088, 1024), (4112, 1024), (5136, 1536), (6672, 32), (6704, 3072)]
        pf = [sb("pf%d" % i, [128, NT], F32) for i in range(3)]
        cv = sb("cv", [128, NT], F32)
        t1 = sb("t1", [128, 512], F32)
        t2 = sb("t2", [128, 512], F32)
        cosb = sb("cosb", [128, 512], F32)
        sinb = sb("sinb", [128, 512], F32)
        cwg = sb("cwg", [128, 12, 5], F32)
        cws = sb("cws", [128, 12, 5], F32)
        cbs = sb("cbs", [128, 12], F32)
        prm = sb("prm", [32, 6], F32)
        psT = psum("psT", [128, 1024])
        psP = [psum("psP%d" % i, [128, 512]) for i in range(2)]
        psR = psum("psR", [128, 512])
        psC = [psum("psC%d" % i, [128, 512]) for i in range(2)]
        xb = [sb("xb%d" % i, [128, NT + 6], BF16) for i in range(2)]
        dgw = [sb("dgw%d" % i, [128, 5, 128], BF16) for i in range(2)]
        for i in range(2):
            pg.op("dve", "memset", [], [("xb", i)], xb[i][:], 0.0)

        pg.dma("sp", ident[:], k.ident[:, :], semkey="ident", writes=["ident"])
        pg.dma("sp", rott[:], k.rott[:, :], semkey="rott", writes=["rott"])
        pg.op("dve", "memset", [], ["ones"], ones[:], 1.0)
        for kk in range(5):
            pg.dma("sp", cwg[:, :, kk], k.gdn_conv_w[li, kk].rearrange("(c p) -> p c", p=128), semkey="cwg",
                   writes=["cwg"], nc_ok=True)
            pg.dma("sp", cws[:, :, kk], k.ssd_conv_w[li, kk].rearrange("(c p) -> p c", p=128), semkey="cws",
                   writes=["cws"], nc_ok=True)
        pg.dma("sp", cbs[:], k.ssd_conv_b[li].rearrange("(c p) -> p c", p=128), semkey="cbs", writes=["cbs"], nc_ok=True)
        pg.op("dve", "memset", [], ["prm"], prm[:], 0.0)
        pg.dma("sp", prm[0:8, 0:1], k.gdn_dt_bias[li].rearrange("a (b o) -> (a b) o", o=1), semkey="prm", writes=["prm"], nc_ok=True)
        pg.dma("sp", prm[0:8, 1:2], k.gdn_a_log[li].rearrange("a (b o) -> (a b) o", o=1), semkey="prm", writes=["prm"], nc_ok=True)
        pg.dma("sp", prm[0:32, 2:3], k.ssd_dt_bias[li].rearrange("a (b o) -> (a b) o", o=1), semkey="prm", writes=["prm"], nc_ok=True)
        pg.dma("sp", prm[0:32, 3:4], k.ssd_a_log[li].rearrange("a (b o) -> (a b) o", o=1), semkey="prm", writes=["prm"], nc_ok=True)
        for c in (1, 3):
            pg.op("act", "activation", ["prm"], ["prm"], out=prm[:, c:c + 1], in_=prm[:, c:c + 1], func=AF.Exp)
            pg.op("dve", "tensor_scalar_mul", ["prm"], ["prm"], out=prm[:, c:c + 1], in0=prm[:, c:c + 1], scalar1=-1.0)
        for s in range(2):
            pg.dma("sp", sc[:, s, :], k.modD[li, s, 4 * 1024:5 * 1024].rearrange("(k p) -> p k", p=128),
                   semkey="sc", writes=["sc"], nc_ok=True)
            pg.dma("sp", sh[:, s, :], k.modD[li, s, 3 * 1024:4 * 1024].rearrange("(k p) -> p k", p=128),
                   semkey="sh", writes=["sh"], nc_ok=True)
        pg.op("dve", "tensor_scalar_add", ["sc"], ["sc"], out=sc[:], in0=sc[:], scalar1=1.0)

        for t in range(NT // 128):
            b = t % 2
            s = 1 if t < C // 128 else 0
            pg.dma("sp", hb[b][:], k.hD[t * 128:(t + 1) * 128, :], semkey=("hb", b), writes=[("hb", b)])
            for kc in range(8):
                pg.op("pe", "transpose", [("hb", b), "ident"], [("psT", kc // 4)],
                      out=psT[:, kc * 128:(kc + 1) * 128], in_=hb[b][:, kc * 128:(kc + 1) * 128], identity=ident[:])
            for kc in range(8):
                pg.op("act", "activation", [("psT", kc // 4), "sc", "sh"], [("uT", t)],
                      out=uT[:, kc, t * 128:(t + 1) * 128], in_=psT[:, kc * 128:(kc + 1) * 128],
                      func=AF.Identity, bias=sh[:, s, kc:kc + 1], scale=sc[:, s, kc:kc + 1])
        uT_keys = [("uT", t) for t in range(NT // 128)]

        chunks = []
        for c in range(12):
            kind = "l2q" if c < 4 else ("l2k" if c < 8 else "conv")
            chunks.append((G_QKV + c * 128, 128, kind, (cwg, c, None)))
        for c in range(4):
            chunks.append((G_Z + c * 128, 128, "raw", None))
        chunks.append((G_B, 8, "sig", None))
        chunks.append((G_A, 8, "gdn_g", None))
        for c in range(4):
            chunks.append((R_Q + c * 128, 128, "rope", 1.0))
        for c in range(4):
            chunks.append((R_K + c * 128, 128, "rope", 128.0 ** -0.5))
        for c in range(8):
            chunks.append((R_V + c * 128, 128, "raw", None))
        for c in range(8):
            chunks.append((S_Z + c * 128, 128, "raw", None))
        for c in range(12):
            chunks.append((S_XBC + c * 128, 128, "conv", (cws, c, cbs)))
        chunks.append((S_DT, 32, "ssd_dt", None))
        for c in range(24):
            chunks.append((BR_G + c * 128, 128, "sig", None))

        heavy = [c for c in chunks if c[2] in ("conv", "l2q", "l2k", "rope")]
        light = [c for c in chunks if c[2] not in ("conv", "l2q", "l2k", "rope")]
        chunks = []
        while heavy or light:
            if heavy:
                chunks.append(heavy.pop(0))
            if light:
                chunks.append(light.pop(0))
        assert C <= 512
        blocks = [(0, C)] + [(t0, min(512, NT - t0)) for t0 in range(C, NT, 512)]
        nconv = 0
        for ci, (c0, w, kind, ex) in enumerate(chunks):
            b = ci % 3
            P, pk = pf[b], ("pf", b)
            xbuf, xbk = xb[nconv % 2], ("xb", nconv % 2)
            stream = 0 if kind in ("conv", "l2q", "l2k", "rope") else 1
            sa_, sw_ = [sg for sg in SEGS if sg[0] <= c0 < sg[0] + sg[1]][0]
            S0 = sa_ + ((c0 - sa_) // 512) * 512
            SW = min(512, sa_ + sw_ - S0)
            if sup_cur[stream][0] != S0:
                j = (sup_cur[stream][1] + 1) % 2
                sup_cur[stream] = (S0, j)
                pg.dma("pool", Wsup[stream][j][:, :, 0:SW], k.mix_w_in[li, :, S0:S0 + SW].rearrange("(k p) n -> p k n", p=128),
                       semkey=("Wsup", stream, j), writes=[("Wsup", stream, j)])
            j = sup_cur[stream][1]
            wk = ("Wsup", stream, j)
            Wv = Wsup[stream][j]
            wo = c0 - S0
            for bi, (t0, n) in enumerate(blocks):
                i = bi % 2
                for kc in range(8):
                    pg.op("pe", "matmul", [wk] + uT_keys, [("psP", i)],
                          psP[i][0:w, 0:n], lhsT=Wv[:, kc, wo:wo + w], rhs=uT[:, kc, t0:t0 + n],
                          start=(kc == 0), stop=(kc == 7))
                if kind in ("conv", "l2q", "l2k"):
                    off = 2 if t0 < C else 4
                    pg.op("act", "copy", [("psP", i)], [xbk], out=xbuf[:, off + t0:off + t0 + n], in_=psP[i][:, 0:n])
                else:
                    pg.op("act", "copy", [("psP", i)], [pk], out=P[0:w, t0:t0 + n], in_=psP[i][0:w, 0:n])
            if kind in ("conv", "l2q", "l2k"):
                cw, cc, cb = ex
                dw = dgw[nconv % 2]
                dwk = ("dgw", nconv % 2)
                for kk in range(5):
                    pg.op("dve", "tensor_scalar_mul", ["ident", "cwg", "cws"], [dwk], out=dw[:, kk, :], in0=ident[:],
                          scalar1=cw[:, cc, kk:kk + 1])
                for bi, (t0, n) in enumerate(blocks):
                    off = 2 if t0 < C else 4
                    i = bi % 2
                    for kk in range(5):
                        pg.op("pe", "matmul", [dwk, xbk], [("psC", i)], psC[i][:, 0:n], lhsT=dw[:, kk, :],
                              rhs=xbuf[:, off + t0 + kk - 2:off + t0 + kk - 2 + n], start=(kk == 0), stop=(kk == 4))
                    if cb is None:
                        pg.op("act", "activation", [("psC", i)], [pk], out=P[:, t0:t0 + n], in_=psC[i][:, 0:n], func=AF.Silu)
                    else:
                        pg.op("act", "activation", [("psC", i), "cbs"], [pk], out=P[:, t0:t0 + n], in_=psC[i][:, 0:n], func=AF.Silu,
                              bias=cb[:, cc:cc + 1])
                nconv += 1
                if kind != "conv":
                    scl = 128.0 ** -0.5 if kind == "l2q" else 1.0
                    pg.op("act", "activation", [pk], ["cv"], out=cv[:], in_=P[:], func=AF.Square)
                    for (t0, n) in blocks:
                        pg.op("pe", "matmul", ["ones", "cv"], ["psR"], psR[:, 0:n], lhsT=ones[:], rhs=cv[:, t0:t0 + n],
                              start=True, stop=True)
                        pg.op("act", "activation", ["psR"], ["t1"], out=t1[:, 0:n], in_=psR[:, 0:n], func=AF.Ln, bias=1e-6)
                        pg.op("act", "activation", ["t1"], ["t1"], out=t1[:, 0:n], in_=t1[:, 0:n], func=AF.Exp, scale=-0.5)
                        pg.op("dve", "scalar_tensor_tensor", [pk, "t1"], [pk], out=P[:, t0:t0 + n],
                              in0=P[:, t0:t0 + n], scalar=scl, in1=t1[:, 0:n], op0=ALU.mult, op1=ALU.mult)
            elif kind == "rope":
                if ex != 1.0:
                    pg.op("dve", "tensor_scalar_mul", [pk], [pk], out=P[:], in0=P[:], scalar1=float(ex))
                for t0 in range(C, NT, 512):
                    n = min(512, NT - t0)
                    pg.dma("sp", cosb[:, 0:n], k.cosT[:, t0 - C:t0 - C + n], semkey="cosb", writes=["cosb"])
                    pg.dma("sp", sinb[:, 0:n], k.sinT[:, t0 - C:t0 - C + n], semkey="sinb", writes=["sinb"])
                    pg.op("pe", "matmul", ["rott", pk], ["psR"], psR[:, 0:n], lhsT=rott[:], rhs=P[:, t0:t0 + n],
                          start=True, stop=True)
                    pg.op("dve", "tensor_tensor", [pk, "cosb"], ["t1"], out=t1[:, 0:n], in0=P[:, t0:t0 + n],
                          in1=cosb[:, 0:n], op=ALU.mult)
                    pg.op("dve", "tensor_tensor", ["psR", "sinb"], ["t2"], out=t2[:, 0:n], in0=psR[:, 0:n],
                          in1=sinb[:, 0:n], op=ALU.mult)
                    pg.op("dve", "tensor_tensor", ["t1", "t2"], [pk], out=P[:, t0:t0 + n], in0=t1[:, 0:n],
                          in1=t2[:, 0:n], op=ALU.add)
            elif kind == "sig":
                pg.op("act", "activation", [pk], [pk], out=P[0:w, :], in_=P[0:w, :], func=AF.Sigmoid)
            elif kind in ("gdn_g", "ssd_dt"):
                col = 0 if kind == "gdn_g" else 2
                pg.op("act", "activation", [pk, "prm"], [pk], out=P[0:w, :], in_=P[0:w, :], func=AF.Exp,
                      bias=prm[0:w, col:col + 1])
                pg.op("act", "activation", [pk], [pk], out=P[0:w, :], in_=P[0:w, :], func=AF.Ln, bias=1.0)
                if kind == "gdn_g":
                    pg.op("dve", "tensor_scalar_mul", [pk, "prm"], [pk], out=P[0:w, :], in0=P[0:w, :],
                          scalar1=prm[0:w, 1:2])
                else:
                    pg.op("dve", "tensor_scalar_mul", [pk, "prm"], ["cv"], out=cv[0:w, :], in0=P[0:w, :],
                          scalar1=prm[0:w, 3:4])
                    pg.dma("sp", k.PF[X_LOGA:X_LOGA + w, :], cv[0:w, :], semkey="cv_st", reads=["cv"])
            pg.dma("sp", k.PF[c0:c0 + w, :], P[0:w, :], semkey=("pf_st", b), reads=[pk])
        pg.barrier()


def build_program(L, C, stop_after=None, dbg=()):
    nc = bass.Bass("TRN2", target_bir_lowering=False)
    k = K()
    k.nc = nc
    k.L, k.C, k.NT = L, C, L + C
    k.stop_scan = (stop_after is not None and stop_after.startswith("scan"))

    def din(name, shape):
        return nc.dram_tensor(name, list(shape), F32, kind="ExternalInput").ap()
    k.x = din("x", [L, D])
    k.ctx = din("ctx", [C, D])
    k.cvec = din("cvec", [2, D])
    k.ident = din("ident", [128, 128])
    k.rott = din("rott", [128, 128])
    k.cosT = din("cosT", [128, L])
    k.sinT = din("sinT", [128, L])
    k.masks = din("masks", [8, 128, 128])
    for name, shape in WEIGHT_SHAPES:
        setattr(k, name, din(name, shape))
    k.out = nc.dram_tensor("out", [L, D], F32, kind="ExternalOutput").ap()

    def scratch(name, shape):
        if name in dbg:
            return nc.dram_tensor(name, list(shape), F32, kind="ExternalOutput").ap()
        return nc.dram_tensor(name, list(shape), F32).ap()
    k.hD = scratch("hD", [k.NT, D])
    k.modD = scratch("modD", [DEPTH, 2, NADA * D])
    k.PF = scratch("PF", [NF, k.NT])
    k.OF = scratch("OF", [2, k.NT, 2048])
    k.dbgX = scratch("dbgX", [128, 1024]) if "dbgX" in dbg else None

    nct = C // 128

    def src_in(t):
        if t < nct:
            return k.ctx[t * 128:(t + 1) * 128, :]
        return k.x[(t - nct) * 128:(t - nct + 1) * 128, :]

    def src_h(t):
        return k.hD[t * 128:(t + 1) * 128, :]

    def dst_out(t):
        assert t >= nct
        return k.out[(t - nct) * 128:(t - nct + 1) * 128, :]

    with ExitStack() as stack:
        k.pg = Prog(nc, stack)

        def body():
            stage_mod(k)
            if stop_after == "mod":
                return
            for li in range(DEPTH):
                last = li == DEPTH - 1
                stage_ffn(k, li, 0, src_in if li == 0 else src_h, src_h, True)
                if stop_after == "ffn%d0" % li:
                    return
                stage_proj(k, li)
                if stop_after == "proj%d" % li:
                    return
                stage_mix(k, li)
                if stop_after in ("mix%d" % li, "scan%d" % li):
                    return
                stage_ffn(k, li, 1, src_h, dst_out if last else src_h, not last)
        body()
        k.pg.barrier()
        k.pg.emit_all()
    return nc


WEIGHT_SHAPES = [
    ("ada_w", [DEPTH, D, NADA * D]), ("ada_b", [DEPTH, NADA * D]), ("ln_g", [DEPTH, 3, D]), ("ln_b", [DEPTH, 3, D]),
    ("ffn_w13", [DEPTH, 2, D, 2 * DFF]), ("ffn_w2", [DEPTH, 2, DFF, D]), ("mix_w_in", [DEPTH, D, N_IN]),
    ("gdn_conv_w", [DEPTH, 5, 1536]), ("gdn_a_log", [DEPTH, 2, 4]), ("gdn_dt_bias", [DEPTH, 2, 4]),
    ("gdn_norm_g", [DEPTH, 128]), ("ret_decay", [DEPTH, 2, 4]), ("ret_norm_g", [DEPTH, 512]),
    ("ssd_conv_w", [DEPTH, 5, 1536]), ("ssd_conv_b", [DEPTH, 1536]), ("ssd_a_log", [DEPTH, 2, 16]),
    ("ssd_dt_bias", [DEPTH, 2, 16]), ("ssd_d", [DEPTH, 16]), ("ssd_norm_g", [DEPTH, 1024]),
    ("w_br_a", [DEPTH, 512, D]), ("w_br_b", [DEPTH, 512, D]), ("w_br_c", [DEPTH, D, D]), ("mix_w_out", [DEPTH, D, D]),
]


def host_consts(L):
    ident = np.eye(128, dtype=np.float32)
    rott = np.zeros((128, 128), np.float32)
    for m in range(64):
        rott[m + 64, m] = -1.0
        rott[m, m + 64] = 1.0
    t = np.arange(L)
    row_id = (t // 64).astype(np.float32)
    col_id = (t % 64).astype(np.float32)
    n_freq = 32
    inv_freq = (np.float32(10000.0) ** (-np.arange(n_freq, dtype=np.float32) / n_freq)).astype(np.float32)
    ang = np.concatenate([row_id[:, None] * inv_freq, col_id[:, None] * inv_freq], axis=-1)
    cosT = np.concatenate([np.cos(ang), np.cos(ang)], axis=1).T.astype(np.float32).copy()
    sinT = np.concatenate([np.sin(ang), np.sin(ang)], axis=1).T.astype(np.float32).copy()
    p = np.arange(128)
    BIG = -30000.0
    masks = np.zeros((8, 128, 128), np.float32)
    masks[6] = ((p[:, None] // 32) == (p[None, :] // 32)).astype(np.float32)
    masks[7] = 1.0 - masks[6]
    masks[4] = np.where(p[None, :] < p[:, None], 0.0, BIG)
    masks[5] = np.where(p[None, :] > p[:, None], 0.0, BIG)
    masks[0] = (p[:, None] <= p[None, :]).astype(np.float32)
    masks[1] = (p[:, None] >= p[None, :]).astype(np.float32)
    masks[2] = np.where(p[None, :] >= p[:, None], 0.0, BIG)
    masks[3] = np.where(p[None, :] <= p[:, None], 0.0, BIG)
    return {"ident": ident, "rott": rott, "cosT": cosT, "sinT": sinT, "masks": masks}


def make_in_map(b, inputs, consts):
    m = {"x": np.ascontiguousarray(inputs["x"][b]), "ctx": np.ascontiguousarray(inputs["ctx"][b]),
         "cvec": np.ascontiguousarray(np.stack([inputs["c"][b], inputs["c_ctx"]]))}
    m.update(consts)
    for name, _ in WEIGHT_SHAPES:
        m[name] = inputs[name]
    return m


def kernel(**inputs):
    inputs = {k_: np.asarray(v, dtype=np.float32) for k_, v in inputs.items()}
    B, L, _ = inputs["x"].shape
    C = inputs["ctx"].shape[1]
    nc = build_program(L, C)
    consts = host_consts(L)
    in_maps = [make_in_map(b, inputs, consts) for b in range(B)]
    res = run_bass_kernel_spmd(nc, in_maps, core_ids=list(range(B)))
    return np.stack([r["out"] for r in res.results], axis=0).astype(np.float32)
```

```python
import contextlib
from contextlib import ExitStack
import numpy as np
import concourse.bass as bass
import concourse.mybir as mybir
from concourse.bass_utils import run_bass_kernel_spmd

F32 = mybir.dt.float32
BF16 = mybir.dt.bfloat16
ALU = mybir.AluOpType
AF = mybir.ActivationFunctionType

D = 1024
DFF = 2816
NADA = 9
DEPTH = 2
ALPHA = float((2 * DEPTH) ** 0.25)
LN_EPS = 1e-5

ENGS = ("pe", "act", "dve", "pool", "sp")
SEM_LIMIT = 30000
import os
CUT = os.environ.get('CUT', '')


class Prog:
    def __init__(self, nc, stack):
        self.nc = nc
        self.stack = stack
        self.ops = {e: [] for e in ENGS}
        self.nsem = 0
        self.esem = {}
        self.ecnt = {}
        for e in ("pe", "act", "dve", "pool"):
            self.esem[e] = self._newsem("e_" + e)
            self.ecnt[e] = 0
        self.dsem = {}
        self.waited = {e: {} for e in ENGS}
        self.last_w = {}
        self.readers = {}
        self.n_ops = 0

    def _newsem(self, name):
        self.nsem += 1
        return self.stack.enter_context(self.nc.semaphore("s%d_%s" % (self.nsem, name)))

    def _need(self, e, dep, waits):
        sem, val, deng = dep
        if deng == "pe" and e == "pe":
            return
        w = self.waited[e].get(id(sem))
        if w is not None and w[1] >= val:
            return
        cur = waits.get(id(sem))
        if cur is None or cur[1] < val:
            waits[id(sem)] = (sem, val)

    def _deps(self, e, reads, writes):
        waits = {}
        for k in reads:
            lw = self.last_w.get(k)
            if lw is not None:
                self._need(e, lw, waits)
        for k in writes:
            lw = self.last_w.get(k)
            if lw is not None:
                self._need(e, lw, waits)
            for r in self.readers.get(k, ()):
                self._need(e, r, waits)
        for sid, (sem, val) in waits.items():
            self.waited[e][sid] = (sem, val)
        return list(waits.values())

    def _commit(self, stamp, reads, writes):
        for k in reads:
            self.readers.setdefault(k, []).append(stamp)
        for k in writes:
            self.last_w[k] = stamp
            self.readers[k] = []

    def op(self, e, meth, reads, writes, *args, **kwargs):
        def emit(eng, meth=meth, args=args, kwargs=kwargs):
            return getattr(eng, meth)(*args, **kwargs)
        waits = self._deps(e, reads, writes)
        if self.ecnt[e] >= SEM_LIMIT:
            self.esem[e] = self._newsem("e_" + e)
            self.ecnt[e] = 0
        self.ecnt[e] += 1
        sem, val = self.esem[e], self.ecnt[e]
        self.ops[e].append((waits, emit, sem, 1))
        self._commit((sem, val, e), reads, writes)
        self.n_ops += 1

    def dma(self, e, out, in_, semkey, reads=(), writes=(), nc_ok=False):
        waits = self._deps(e, reads, writes)
        ds = self.dsem.get(semkey)
        if ds is None or ds[1] * 16 >= SEM_LIMIT:
            ds = [self._newsem("d"), 0]
            self.dsem[semkey] = ds
        ds[1] += 1
        sem, val = ds[0], ds[1] * 16

        def emit(eng, out=out, in_=in_, nc_ok=nc_ok):
            if nc_ok:
                return eng.dma_start(out=out, in_=in_, allow_slow_non_contiguous=True)
            return eng.dma_start(out=out, in_=in_)
        self.ops[e].append((waits, emit, sem, 16))
        self._commit((sem, val, "dma"), reads, writes)
        self.n_ops += 1

    def barrier(self):
        stamps = []
        for e in ("pe", "act", "dve", "pool"):
            if self.ecnt[e] > 0:
                stamps.append((self.esem[e], self.ecnt[e], "x"))
        for k, ds in self.dsem.items():
            stamps.append((ds[0], ds[1] * 16, "dma"))
        for e in ENGS:
            waits = {}
            for st in stamps:
                self._need(e, st, waits)
            for sid, (sem, val) in waits.items():
                self.waited[e][sid] = (sem, val)
            if waits:
                self.ops[e].append((list(waits.values()), None, None, 0))
        self.last_w = {}
        self.readers = {}

    def emit_all(self):
        with self.nc.Block() as block:
            def run(e):
                def f(eng):
                    for waits, emit, sem, inc in self.ops[e]:
                        for (ws, wv) in waits:
                            eng.wait_ge(ws, wv)
                        if emit is not None:
                            emit(eng).then_inc(sem, inc)
                return f
            block.tensor(run("pe"))
            block.scalar(run("act"))
            block.vector(run("dve"))
            block.gpsimd(run("pool"))
            block.sync(run("sp"))


class K:
    pass


def _consts_np():
    c = np.zeros((128, 128), np.float32)
    np.fill_diagonal(c, 1.0)
    return {"ident": c}


def stage_mod(k):
    pg, nc = k.pg, k.nc
    with ExitStack() as st:
        cT = st.enter_context(nc.sbuf_tensor("cT", [128, 8, 2], F32))
        sT = st.enter_context(nc.sbuf_tensor("sT", [128, 8, 2], F32))
        aw = [st.enter_context(nc.sbuf_tensor("aw%d" % i, [128, 8, 1024], F32)) for i in range(2)]
        ab = st.enter_context(nc.sbuf_tensor("ab", [2, 1024], F32))
        mrow = [st.enter_context(nc.sbuf_tensor("mrow%d" % i, [2, 1024], F32)) for i in range(2)]
        ps = [st.enter_context(nc.psum_tensor("psm%d" % i, [2, 512], F32)) for i in range(2)]
        for s in range(2):
            pg.dma("sp", cT[:, :, s], k.cvec[s].rearrange("(k p) -> p k", p=128), semkey="cT",
                   writes=["cT"], nc_ok=True)
        pg.op("act", "activation", ["cT"], ["sT"], out=sT[:], in_=cT[:], func=AF.Silu)
        it = 0
        for l in range(DEPTH):
            for j in range(NADA):
                b = it % 2
                it += 1
                pg.dma("sp", aw[b][:], k.ada_w[l, :, j * 1024:(j + 1) * 1024].rearrange("(k p) n -> p k n", p=128),
                       semkey=("aw", b), writes=[("aw", b)])
                for s in range(2):
                    pg.dma("sp", ab[s:s + 1, :], k.ada_b[l:l + 1, j * 1024:(j + 1) * 1024], semkey="ab",
                           writes=["ab"])
                for h in range(2):
                    for kc in range(8):
                        pg.op("pe", "matmul", ["sT", ("aw", b)], [("psm", h)],
                              ps[h][:], lhsT=sT[:, kc, :], rhs=aw[b][:, kc, h * 512:(h + 1) * 512],
                              start=(kc == 0), stop=(kc == 7))
                    pg.op("dve", "tensor_tensor", [("psm", h), "ab"], [("mrow", b, h)],
                          out=mrow[b][:, h * 512:(h + 1) * 512], in0=ps[h][:], in1=ab[:, h * 512:(h + 1) * 512],
                          op=ALU.add)
                pg.dma("pool", k.modD[l, :, j * 1024:(j + 1) * 1024], mrow[b][:], semkey=("mrow", b),
                       reads=[("mrow", b, 0), ("mrow", b, 1)])
        pg.barrier()


def stage_ffn(k, li, fi, src, dst, do_ctx):
    pg, nc = k.pg, k.nc
    C, NT = k.C, k.NT
    j_shift, j_scale, j_gate = (0, 1, 2) if fi == 0 else (6, 7, 8)
    lni = 0 if fi == 0 else 2
    pfx = "f%d%d_" % (li, fi)
    with ExitStack() as st:
        def sb(name, shape, dt):
            return st.enter_context(nc.sbuf_tensor(pfx + name, shape, dt))

        def psum(name, shape):
            return st.enter_context(nc.psum_tensor(pfx + name, shape, F32))
        W13 = sb("W13", [128, 8, 2 * DFF], BF16)
        W2 = sb("W2", [128, 22, D], BF16)
        ident = sb("ident", [128, 128], F32)
        G = sb("G", [128, D], F32)
        LNG = sb("LNG", [128, D], F32)
        LNB = sb("LNB", [128, D], F32)
        sc = sb("sc", [128, 2, 8], F32)
        sh = sb("sh", [128, 2, 8], F32)
        hb = [sb("hb%d" % i, [128, D], F32) for i in range(2)]
        uT = [sb("uT%d" % i, [128, 8, 512], BF16) for i in range(2)]
        gT = sb("gT", [128, 22, 512], BF16)
        sa = [sb("sa%d" % i, [128, 512], F32) for i in range(2)]
        tmp = sb("tmp", [128, D], F32)
        lnw = ln_scratch(sb)
        psT = psum("psT", [128, 1024])
        psA = [psum("psA%d" % i, [128, 512]) for i in range(2)]
        psB = [psum("psB%d" % i, [128, 512]) for i in range(2)]
        psY = psum("psY", [128, 1024])

        pg.dma("sp", ident[:], k.ident[:, :], semkey="ident", writes=["ident"])
        for kc in range(8):
            pg.dma("pool", W13[:, kc, :], k.ffn_w13[li, fi, kc * 128:(kc + 1) * 128, :], semkey="W13",
                   writes=["W13"])
        for f0 in range(0, 22, 6):
            f1 = min(22, f0 + 6)
            pg.dma("pool", W2[:, f0:f1, :],
                   k.ffn_w2[li, fi, f0 * 128:f1 * 128, :].rearrange("(f p) d -> p f d", p=128),
                   semkey="W2", writes=["W2"])
        pg.dma("sp", LNG[:], k.ln_g[li, lni:lni + 1, :].partition_broadcast(128), semkey="LNG", writes=["LNG"])
        pg.dma("sp", LNB[:], k.ln_b[li, lni:lni + 1, :].partition_broadcast(128), semkey="LNB", writes=["LNB"])
        for s in range(2):
            pg.dma("sp", sc[:, s, :], k.modD[li, s, j_scale * 1024:(j_scale + 1) * 1024].rearrange("(k p) -> p k", p=128),
                   semkey="sc", writes=["sc"], nc_ok=True)
            pg.dma("sp", sh[:, s, :], k.modD[li, s, j_shift * 1024:(j_shift + 1) * 1024].rearrange("(k p) -> p k", p=128),
                   semkey="sh", writes=["sh"], nc_ok=True)
        pg.op("dve", "tensor_scalar_add", ["sc"], ["sc"], out=sc[:], in0=sc[:], scalar1=1.0)

        blocks = []
        if do_ctx:
            blocks.append((0, C // 128, 1))
        for t0 in range(C // 128, NT // 128, 4):
            blocks.append((t0, min(4, NT // 128 - t0), 0))

        state = {"cur_g": None, "nld": 0}

        def phA(blk, ub):
            t0, ntl, s = blk
            for tl in range(ntl):
                b = state["nld"] % 2
                state["nld"] += 1
                pg.dma("sp", hb[b][:], src(t0 + tl), semkey=("hb", b), writes=[("hb", b)])
                for kc in range(8):
                    pg.op("pe", "transpose", [("hb", b), "ident"], [("psT", kc // 4)],
                          out=psT[:, kc * 128:(kc + 1) * 128], in_=hb[b][:, kc * 128:(kc + 1) * 128],
                          identity=ident[:])
                for kc in range(8):
                    pg.op("act", "activation", [("psT", kc // 4), "sc", "sh"], [("uT", ub, tl, kc)],
                          out=uT[ub][:, kc, tl * 128:(tl + 1) * 128], in_=psT[:, kc * 128:(kc + 1) * 128],
                          func=AF.Identity, bias=sh[:, s, kc:kc + 1], scale=sc[:, s, kc:kc + 1])

        def phB(blk, ub):
            t0, ntl, s = blk
            N = ntl * 128
            for f in range(22):
                i = f % 2
                for (ps_, c0, key) in ((psA[i], f * 128, ("psA", i)), (psB[i], DFF + f * 128, ("psB", i))):
                    for kc in range(8):
                        pg.op("pe", "matmul", ["W13"] + [("uT", ub, tl, kc) for tl in range(ntl)], [key],
                              ps_[:, 0:N], lhsT=W13[:, kc, c0:c0 + 128], rhs=uT[ub][:, kc, 0:N],
                              start=(kc == 0), stop=(kc == 7))
                pg.op("act", "activation", [("psA", i)], [("sa", i)],
                      out=sa[i][:, 0:N], in_=psA[i][:, 0:N], func=AF.Silu)
                pg.op("dve", "tensor_tensor", [("sa", i), ("psB", i)], [("gT", f)],
                      out=gT[:, f, 0:N], in0=sa[i][:, 0:N], in1=psB[i][:, 0:N], op=ALU.mult)

        def phC(blk):
            t0, ntl, s = blk
            if state["cur_g"] != s:
                pg.dma("sp", G[:], k.modD[li, s:s + 1, j_gate * 1024:(j_gate + 1) * 1024].partition_broadcast(128),
                       semkey="G", writes=["G"])
                pg.op("act", "mul", ["G"], ["G"], out=G[:], in_=G[:], mul=0.5)
                state["cur_g"] = s
            for tl in range(ntl):
                b = state["nld"] % 2
                state["nld"] += 1
                hk = ("hb", b)
                pg.dma("sp", hb[b][:], src(t0 + tl), semkey=hk, writes=[hk])
                for half in range(2):
                    hs = slice(half * 512, (half + 1) * 512)
                    for f in range(22):
                        pg.op("pe", "matmul", ["W2", ("gT", f)], [("psY", half)],
                              psY[:, hs], lhsT=gT[:, f, tl * 128:(tl + 1) * 128], rhs=W2[:, f, hs],
                              start=(f == 0), stop=(f == 21))
                    pg.op("dve", "tensor_tensor", [("psY", half), "G"], [("tmp", half)],
                          out=tmp[:, hs], in0=psY[:, hs], in1=G[:, hs], op=ALU.mult)
                pg.op("dve", "scalar_tensor_tensor", [hk, ("tmp", 0), ("tmp", 1)], [hk],
                      out=hb[b][:], in0=hb[b][:], scalar=ALPHA, in1=tmp[:], op0=ALU.mult, op1=ALU.add)
                layer_norm_tile(k, hb[b], hk, lnw, LNG, LNB)
                pg.dma("pool", dst(t0 + tl), hb[b][:], semkey=("hb_st", b), reads=[hk])

        phA(blocks[0], 0)
        for n, blk in enumerate(blocks):
            phB(blk, n % 2)
            if n + 1 < len(blocks):
                phA(blocks[n + 1], (n + 1) % 2)
            phC(blk)
        pg.barrier()


def ln_scratch(sb):
    return (sb("stt", [128, 12], F32), sb("mv", [128, 2], F32), sb("rstd", [128, 1], F32))


def layer_norm_tile(k, h, hk, lnw, LNG, LNB):
    pg = k.pg
    stt, mv, rstd = lnw
    for c in range(2):
        pg.op("dve", "bn_stats", [hk], [("stt", c)], out=stt[:, c * 6:(c + 1) * 6], in_=h[:, c * 512:(c + 1) * 512])
    pg.op("dve", "bn_aggr", [("stt", 0), ("stt", 1)], ["mv"], out=mv[:], in_=stt[:])
    if CUT == "L1":
        return
    pg.op("act", "activation", ["mv"], ["rstd"], out=rstd[:], in_=mv[:, 1:2], func=AF.Sqrt, bias=LN_EPS)
    pg.op("dve", "reciprocal", ["rstd"], ["rstd"], out=rstd[:], in_=rstd[:])
    if CUT == "L2":
        return
    pg.op("dve", "tensor_scalar", [hk, "mv", "rstd"], [hk], out=h[:], in0=h[:], scalar1=mv[:, 0:1],
          scalar2=rstd[:, 0:1], op0=ALU.subtract, op1=ALU.mult)
    if CUT == "L3":
        return
    e2 = "dve"
    pg.op(e2, "tensor_tensor", [hk, "LNG"], [hk], out=h[:], in0=h[:], in1=LNG[:], op=ALU.mult)
    pg.op(e2, "tensor_tensor", [hk, "LNB"], [hk], out=h[:], in0=h[:], in1=LNB[:], op=ALU.add)


G_QKV, G_Z, G_B, G_A = 0, 1536, 2048, 2056
R_Q, R_K, R_V, R_G = 2064, 2576, 3088, 3600
S_Z, S_XBC, S_DT, BR_G = 4112, 5136, 6672, 6704
N_IN = 9776
X_LOGA = 9776
NF = 9808


def stage_mix(k, li):
    stage_scan(k, li)
    if getattr(k, "stop_scan", False):
        return
    stage_finish(k, li)


def run_lanes(tasks, nlanes):
    tasks = list(tasks)
    lanes = [None] * nlanes
    while True:
        busy = False
        for ln in range(nlanes):
            if lanes[ln] is None and tasks:
                lanes[ln] = tasks.pop(0)(ln)
            g = lanes[ln]
            if g is None:
                continue
            busy = True
            try:
                next(g)
            except StopIteration:
                lanes[ln] = None
        if not busy and not tasks:
            break


NL = 8
NLH = 4


def stage_scan(k, li):
    pg, nc = k.pg, k.nc
    C, NT = k.C, k.NT
    nch, nct = NT // 128, C // 128
    pfx = "s%d_" % li
    with ExitStack() as st:
        def sb(name, shape, dt=F32):
            return st.enter_context(nc.sbuf_tensor(pfx + name, shape, dt))

        def psum(name, shape):
            return st.enter_context(nc.psum_tensor(pfx + name, shape, F32))

        def lanes(name, shape, dt=F32, n=NL):
            return [sb("%s_l%d" % (name, i), shape, dt) for i in range(n)]
        ident = sb("ident", [128, 128])
        ones = sb("ones", [128, 128])
        msk = sb("msk", [128, 8, 128])
        rdec = sb("rdec", [128, 8])
        FA2 = [sb("FA%d" % i, [128, 12, 128]) for i in range(2)]
        FR2 = [sb("FR%d" % i, [128, 12, 128]) for i in range(2)]
        FS2 = [sb("FS%d" % i, [128, 12, 128]) for i in range(2)]
        smF2 = [sb("smF%d" % i, [128, 128]) for i in range(2)]
        sm2 = [sb("sm%d" % i, [128, 80]) for i in range(2)]
        TK2 = [sb("TK%d" % i, [128, 26, 128]) for i in range(2)]
        scG2 = [sb("scG%d" % i, [128, 2, 128]) for i in range(2)]
        dec2 = [sb("dec%d" % i, [128, 6, 24]) for i in range(2)]
        nbg2 = [sb("nbg%d" % i, [128, 8]) for i in range(2)]
        dg = lanes("dg", [128, 128])
        dT = lanes("dT", [128, 128])
        EA = lanes("EA", [128, 128], n=NLH)
        scT = lanes("scT", [128, 128])
        PT = lanes("PT", [128, 128], BF16)
        Mk = lanes("Mk", [128, 128], n=NLH)
        Pk = lanes("Pk", [128, 128], n=NLH)
        MD = lanes("MD", [128, 5, 128], n=NLH)
        PD = lanes("PD", [128, 5, 128], n=NLH)
        MO = lanes("MO", [128, 128], n=NLH)
        Yb = lanes("Yb", [128, 256], n=NLH)
        Qm = lanes("Qm", [128, 128], n=NLH)
        FT = lanes("FT", [128, 128], n=NLH)
        X = lanes("X", [128, 256], n=NLH)
        wkT = lanes("wkT", [128, 128], BF16, n=NLH)
        u = lanes("u", [128, 128], BF16, n=NLH)
        xd = lanes("xd", [128, 64], BF16)
        vw = lanes("vw", [128, 128], BF16)
        oa = lanes("oa", [128, 128])
        otok = [sb("otok%d" % i, [128, 2048]) for i in range(2)]
        g_dg = [sb("g_dg%d" % i, [128, 8, 128]) for i in range(2)]
        g_dT = [sb("g_dT%d" % i, [128, 4, 128]) for i in range(2)]
        g_PT = [sb("g_PT%d" % i, [128, 8, 128], BF16) for i in range(2)]
        g_xd = [sb("g_xd%d" % i, [128, 8, 64], BF16) for i in range(2)]
        g_vw = [sb("g_vw%d" % i, [128, 8, 64], BF16) for i in range(2)]
        g_oa = [sb("g_oa%d" % i, [128, 8, 64]) for i in range(2)]
        g_tm = [sb("g_tm%d" % i, [128, 8, 64]) for i in range(2)]
        Sg = sb("Sg", [128, 4, 128])
        Sr = sb("Sr", [128, 4, 128])
        Ss = sb("Ss", [128, 16, 64])
        Sgb = sb("Sgb", [128, 4, 128], BF16)
        Srb = sb("Srb", [128, 4, 128], BF16)
        Ssb = sb("Ssb", [128, 16, 64], BF16)
        QK2 = [sb("QK%d" % i, [128, 20, 128], BF16) for i in range(2)]
        TB2 = [sb("TB%d" % i, [128, 14, 128], BF16) for i in range(2)]
        PB = [psum("pb%d" % i, [128, 512]) for i in range(8)]

        pg.dma("sp", ident[:], k.ident[:, :], semkey="ident", writes=["ident"])
        pg.dma("sp", msk[:], k.masks.rearrange("m p f -> p m f"), semkey="msk", writes=["msk"])
        pg.op("dve", "memset", [], ["ones"], ones[:], 1.0)
        pg.dma("sp", rdec[:], k.ret_decay[li:li + 1].rearrange("o a b -> o (a b)").partition_broadcast(128),
               semkey="rdec", writes=["rdec"])
        pg.op("act", "activation", ["rdec"], ["rdec"], out=rdec[:], in_=rdec[:], func=AF.Exp)
        pg.op("dve", "tensor_scalar_mul", ["rdec"], ["rdec"], out=rdec[:], in0=rdec[:], scalar1=-1.0)
        for i in range(2):
            pg.op("dve", "memset", [], [("smF", i)], smF2[i][:], 0.0)

        def preamble(ln, ch, pb, di):
            ts = slice(ch * 128, ch * 128 + 128)
            tri = msk[:, di, :]
            FA, FR, FS, smF, sm, TK, scG, dec, nbg = (FA2[pb], FR2[pb], FS2[pb], smF2[pb], sm2[pb], TK2[pb],
                                                      scG2[pb], dec2[pb], nbg2[pb])
            kFA, kFR, kFS, ksmF, ksm, kdec = ("FA", pb), ("FR", pb), ("FS", pb), ("smF", pb), ("sm", pb), ("dec", pb)
            QK, TB, kQK = QK2[pb], TB2[pb], ("QK", pb)
            gall, Gi, Gt, eG, wj, cc = (dec[:, j, :] for j in range(6))
            nb, bg = nbg[:, 0:4], nbg[:, 4:8]
            yield pg.dma("sp", FA[:], k.PF[0:1536, ts].rearrange("(c p) t -> p c t", p=128), semkey=kFA, writes=[kFA])
            yield pg.dma("sp", FR[:], k.PF[R_Q:R_Q + 1536, ts].rearrange("(c p) t -> p c t", p=128), semkey=kFR, writes=[kFR])
            yield pg.dma("sp", FS[:], k.PF[S_XBC:S_XBC + 1536, ts].rearrange("(c p) t -> p c t", p=128), semkey=kFS, writes=[kFS])
            yield pg.dma("sp", smF[0:16, :], k.PF[G_B:G_B + 16, ts], semkey=ksmF, writes=[ksmF])
            yield pg.dma("sp", smF[16:48, :], k.PF[S_DT:S_DT + 32, ts], semkey=ksmF, writes=[ksmF])
            yield pg.dma("sp", smF[48:80, :], k.PF[X_LOGA:X_LOGA + 32, ts], semkey=ksmF, writes=[ksmF])
            yield pg.op("act", "copy", [kFA], [kQK], out=QK[:, 0:8, :], in_=FA[:, 0:8, :])
            yield pg.op("act", "copy", [kFR], [kQK], out=QK[:, 8:16, :], in_=FR[:, 0:8, :])
            yield pg.op("act", "copy", [kFS], [kQK], out=QK[:, 16:20, :], in_=FS[:, 8:12, :])
            jobs = [(smF[:], [ksmF], None)]
            for c in range(4):
                jobs.append((FA[:, 4 + c, :], [kFA], c))
            for c in range(4):
                jobs.append((FA[:, 8 + c, :], [kFA], 4 + c))
            for c in range(4):
                jobs.append((FR[:, 4 + c, :], [kFR], 8 + c))
            for c in range(4):
                jobs.append((FR[:, 8 + c, :], [kFR], 12 + c))
            for c in range(10):
                jobs.append((FS[:, c, :], [kFS], 16 + c))
            bank_i = 0
            src0, rk0, _ = jobs[0]
            bk = PB[ln]
            yield pg.op("pe", "transpose", rk0 + ["ident"], [("pb", ln)], out=bk[:, 0:128], in_=src0, identity=ident[:])
            yield pg.op("act", "copy", [("pb", ln)], [ksm], out=sm[:], in_=bk[:, 0:80])
            rest = jobs[1:]
            for g0 in range(0, len(rest), 4):
                grp = rest[g0:g0 + 4]
                bank_i += 1
                bk, bkk = PB[ln], ("pb", ln)
                for j, (src, rk, slot) in enumerate(grp):
                    yield pg.op("pe", "transpose", rk + ["ident"], [bkk], out=bk[:, j * 128:(j + 1) * 128], in_=src, identity=ident[:])
                s0 = grp[0][2]
                yield pg.op("act", "copy", [bkk], [("TK", pb, s_) for (_, _, s_) in grp],
                      out=TK[:, s0:s0 + len(grp), :], in_=bk[:, 0:len(grp) * 128].rearrange("p (c t) -> p c t", t=128))
                tb0 = {0: 0, 8: 4, 12: 8, 24: 12}.get(s0)
                if tb0 is not None:
                    yield pg.op("act", "copy", [bkk], [("TB", pb, tb0 // 4)],
                          out=TB[:, tb0:tb0 + len(grp), :], in_=bk[:, 0:len(grp) * 128].rearrange("p (c t) -> p c t", t=128))
            yield pg.op("dve", "tensor_copy", [ksm], [kdec], out=gall[:, 0:4], in_=sm[:, 8 + di * 4:12 + di * 4])
            yield pg.op("dve", "tensor_copy", ["rdec"], [kdec], out=gall[:, 4:8], in_=rdec[:, di * 4:di * 4 + 4])
            yield pg.op("dve", "tensor_copy", [ksm], [kdec], out=gall[:, 8:24], in_=sm[:, 48 + di * 16:64 + di * 16])
            bk, bkk = PB[ln], ("pb", ln)
            yield pg.op("pe", "matmul", ["msk", kdec], [bkk], bk[:, 0:24], lhsT=tri, rhs=gall, start=True, stop=True)
            yield pg.op("act", "copy", [bkk], [kdec], out=Gi, in_=bk[:, 0:24])
            bk, bkk = PB[ln], ("pb", ln)
            yield pg.op("pe", "matmul", ["ones", kdec], [bkk], bk[:, 0:24], lhsT=ones[:], rhs=gall, start=True, stop=True)
            yield pg.op("act", "copy", [bkk], [kdec], out=Gt, in_=bk[:, 0:24])
            yield pg.op("act", "activation", [kdec], [kdec], out=eG, in_=Gi, func=AF.Exp)
            yield pg.op("act", "activation", [kdec], [kdec], out=cc, in_=Gt, func=AF.Exp)
            yield pg.op("dve", "tensor_tensor", [kdec], [kdec], out=wj, in0=Gt, in1=Gi, op=ALU.subtract)
            yield pg.op("act", "activation", [kdec], [kdec], out=wj, in_=wj, func=AF.Exp)
            yield pg.op("dve", "tensor_scalar_mul", [ksm], [("nbg", pb)], out=nb, in0=sm[:, di * 4:di * 4 + 4], scalar1=-1.0)
            yield pg.op("dve", "tensor_tensor", [ksm, kdec], [("nbg", pb)], out=bg, in0=sm[:, di * 4:di * 4 + 4], in1=eG[:, 0:4],
                  op=ALU.mult)
            for g in range(2):
                bk, bkk = PB[ln], ("pb", ln)
                yield pg.op("pe", "matmul", [kQK], [bkk], bk[:, 0:128], lhsT=QK[:, 16 + g, :], rhs=QK[:, 18 + g, :], start=True, stop=True)
                yield pg.op("act", "copy", [bkk], [("scG", pb, g)], out=scG[:, g, :], in_=bk[:, 0:128])

        nout = 0
        for di in range(2):
            order = list(range(nch)) if di == 0 else (list(range(nct - 1, -1, -1)) + list(range(nch - 1, nct - 1, -1)))
            tri = msk[:, di, :]
            mD = msk[:, 2 + di, :]
            mA = msk[:, 4 + di, :]
            pg.op("dve", "memset", [], [("Sg", h_) for h_ in range(4)], Sg[:], 0.0)
            pg.op("dve", "memset", [], [("Sr", h_) for h_ in range(4)], Sr[:], 0.0)
            pg.op("dve", "memset", [], [("Ss", h_) for h_ in range(16)], Ss[:], 0.0)
            pg.op("dve", "memset", [], [("Sgb", h_) for h_ in range(4)], Sgb[:], 0.0)
            pg.op("dve", "memset", [], [("Srb", h_) for h_ in range(4)], Srb[:], 0.0)
            pg.op("dve", "memset", [], [("Ssb", h_) for h_ in range(16)], Ssb[:], 0.0)
            for _ in preamble(7, order[0], nout % 2, di):
                pass
            for oi, ch in enumerate(order):
                t0 = ch * 128
                ts = slice(t0, t0 + 128)
                pb = nout % 2
                FA, FR, FS, smF, sm, TK, scG, dec, nbg = (FA2[pb], FR2[pb], FS2[pb], smF2[pb], sm2[pb], TK2[pb],
                                                          scG2[pb], dec2[pb], nbg2[pb])
                kFA, kFR, kFS, ksmF, ksm, kdec = ("FA", pb), ("FR", pb), ("FS", pb), ("smF", pb), ("sm", pb), ("dec", pb)
                QK, TB, kQK = QK2[pb], TB2[pb], ("QK", pb)
                gall, Gi, Gt, eG, wj, cc = (dec[:, j, :] for j in range(6))
                nb, bg = nbg[:, 0:4], nbg[:, 4:8]
                ot = otok[pb]
                nout += 1
                kdecs = [kdec, ("nbg", pb), ksm]

                def head(ln, hc, qT, kT, ktok, vtok, vkeys, dv, S, skey, ocol, sc_src, qkeys, qTb, kTb, ktokb, vtokb, Sb, sbkey,
                         gdn_h=None, xd_src=None, QK=QK, TB=TB, kQK=kQK, ot=ot, pb=pb, Gi=Gi, eG=eG, wj=wj, cc=cc, nb=nb, bg=bg, sm=sm, kdecs=kdecs):
                    b0, b1 = PB[ln][:, 0:256], PB[ln][:, 256:512]
                    k0 = k1 = ("pb", ln)
                    L = lambda n: (n, ln)
                    if xd_src is not None:
                        xsrc, xkeys, dcol = xd_src
                        yield pg.op("dve", "tensor_scalar_mul", xkeys + kdecs, [L("xd")], out=xd[ln][:], in0=xsrc, scalar1=sm[:, dcol:dcol + 1])
                    yield pg.op("dve", "tensor_scalar_mul", ["ident"] + kdecs, [L("dg")], out=dg[ln][:], in0=ident[:], scalar1=Gi[:, hc:hc + 1])
                    yield pg.op("pe", "matmul", ["ones", L("dg")], [k0], b0[:, 0:128], lhsT=ones[:], rhs=dg[ln][:], start=True, stop=True)
                    yield pg.op("dve", "scalar_tensor_tensor", [k0, "msk"] + kdecs, [L("dT")], out=dT[ln][:], in0=b0[:, 0:128],
                                scalar=Gi[:, hc:hc + 1], in1=mD, op0=ALU.subtract, op1=ALU.add)
                    if gdn_h is not None:
                        h = gdn_h
                        yield pg.op("dve", "scalar_tensor_tensor", [k0, "msk"] + kdecs, [L("EA")], out=EA[ln][:], in0=b0[:, 0:128],
                                    scalar=Gi[:, hc:hc + 1], in1=mA, op0=ALU.subtract, op1=ALU.subtract)
                    yield pg.op("act", "activation", [L("dT")], [L("dT")], out=dT[ln][:], in_=dT[ln][:], func=AF.Exp)
                    if gdn_h is not None:
                        yield pg.op("act", "activation", [L("EA")], [L("EA")], out=EA[ln][:], in_=EA[ln][:], func=AF.Exp, scale=-1.0)
                        yield pg.op("pe", "matmul", qkeys, [k1], b1[:, 0:128], lhsT=kT, rhs=kT, start=True, stop=True)
                        yield pg.op("dve", "scalar_tensor_tensor", [k1, L("EA")] + kdecs, [L("Pk")], out=Pk[ln][:], in0=b1[:, 0:128],
                                    scalar=nb[:, h:h + 1], in1=EA[ln][:], op0=ALU.mult, op1=ALU.mult)
                        yield pg.op("pe", "transpose", [L("Pk"), "ident"], [k0], out=b0[:, 0:128], in_=Pk[ln][:], identity=ident[:])
                        yield pg.op("act", "copy", [k0], [L("Mk")], out=Mk[ln][:], in_=b0[:, 0:128])
                        yield pg.op("dve", "tensor_scalar_mul", vkeys + kdecs, [L("X")], out=X[ln][:, 0:128], in0=vtok,
                                    scalar1=sm[:, di * 4 + h:di * 4 + h + 1])
                        yield pg.op("dve", "tensor_scalar_mul", vkeys + kdecs, [L("X")], out=X[ln][:, 128:256], in0=ktok, scalar1=bg[:, h:h + 1])
                        yield pg.op("dve", "tensor_tensor", [L("Pk"), "msk"], [L("PD0")], out=PD[ln][:, 0, :], in0=Pk[ln][:], in1=msk[:, 6, :], op=ALU.mult)
                        yield pg.op("dve", "tensor_tensor", [L("Mk"), "msk"], [L("MD0")], out=MD[ln][:, 0, :], in0=Mk[ln][:], in1=msk[:, 6, :], op=ALU.mult)
                        yield pg.op("dve", "tensor_tensor", [L("Pk"), "msk"], [L("MO")], out=MO[ln][:], in0=Pk[ln][:], in1=msk[:, 7, :], op=ALU.mult)
                        yield pg.op("dve", "tensor_tensor", [L("MD0"), "ident"], [L("Q")], out=Qm[ln][:], in0=MD[ln][:, 0, :], in1=ident[:], op=ALU.add)
                        for lev in range(4):
                            md, pd = L("MD%d" % lev), L("PD%d" % lev)
                            yield pg.op("pe", "matmul", [md, pd], [k1], b1[:, 0:128], lhsT=MD[ln][:, lev, :], rhs=PD[ln][:, lev, :],
                                        start=True, stop=True)
                            if lev < 3:
                                yield pg.op("pe", "matmul", [md, pd], [k0], b0[:, 0:128], lhsT=PD[ln][:, lev, :], rhs=MD[ln][:, lev, :],
                                            start=True, stop=True)
                            yield pg.op("act", "copy", [k1], [L("PD%d" % (lev + 1))], out=PD[ln][:, lev + 1, :], in_=b1[:, 0:128])
                            if lev < 3:
                                yield pg.op("act", "copy", [k0], [L("MD%d" % (lev + 1))], out=MD[ln][:, lev + 1, :], in_=b0[:, 0:128])
                            yield pg.op("pe", "matmul", [L("PD%d" % (lev + 1)), L("Q")], [k1], b1[:, 0:128], lhsT=PD[ln][:, lev + 1, :], rhs=Qm[ln][:],
                                        start=True, stop=True)
                            yield pg.op("dve", "tensor_tensor", [L("Q"), k1], [L("Q")], out=Qm[ln][:], in0=Qm[ln][:], in1=b1[:, 0:128], op=ALU.add)
                        yield pg.op("pe", "matmul", [L("MO"), L("Q")], [k0], b0[:, 0:128], lhsT=MO[ln][:], rhs=Qm[ln][:], start=True, stop=True)
                        yield pg.op("act", "copy", [k0], [L("FT")], out=FT[ln][:], in_=b0[:, 0:128])
                        yield pg.op("pe", "matmul", [L("Q"), L("X")], [k1], b1[:, 0:256], lhsT=Qm[ln][:], rhs=X[ln][:], start=True, stop=True)
                        yield pg.op("act", "copy", [k1], [L("Yb")], out=Yb[ln][:], in_=b1[:, 0:256])
                        for it in range(3):
                            src, srck = (Yb[ln], L("Yb")) if it == 0 else (X[ln], L("X"))
                            bb, kk_ = (b0, k0) if it % 2 == 0 else (b1, k1)
                            yield pg.op("pe", "matmul", [L("FT"), srck], [kk_], bb[:, 0:256], lhsT=FT[ln][:], rhs=src[:], start=True, stop=True)
                            yield pg.op("dve", "tensor_tensor", [L("Yb"), kk_], [L("X")], out=X[ln][:], in0=Yb[ln][:], in1=bb[:, 0:256], op=ALU.add)
                        yield pg.op("pe", "transpose", [L("X"), "ident"], [k0], out=b0[:, 0:128], in_=X[ln][:, 128:256], identity=ident[:])
                        yield pg.op("act", "copy", [k0], [L("wkT")], out=wkT[ln][:], in_=b0[:, 0:128])
                        yield pg.op("pe", "matmul", [L("wkT"), sbkey], [k1], b1[:, 0:128], lhsT=wkT[ln][:], rhs=Sb, start=True, stop=True)
                        yield pg.op("dve", "tensor_tensor", [L("X"), k1], [L("u")], out=u[ln][:], in0=X[ln][:, 0:128], in1=b1[:, 0:128], op=ALU.subtract)
                        vtok, vkeys = u[ln][:], [L("u")]
                        vtokb = u[ln][:]
                    if xd_src is not None:
                        vtok, vkeys = xd[ln][:], [L("xd")] + vkeys
                        vtokb = xd[ln][:]
                    if sc_src is None:
                        yield pg.op("pe", "matmul", [kQK], [k1], b1[:, 0:128], lhsT=kTb, rhs=qTb, start=True, stop=True)
                        yield pg.op("act", "copy", [k1], [L("scT")], out=scT[ln][:], in_=b1[:, 0:128])
                        sc_ap, sc_k = scT[ln][:], L("scT")
                    else:
                        sc_ap, sc_k = sc_src
                    yield pg.op("dve", "tensor_tensor", [sc_k, L("dT")], [L("PT")], out=PT[ln][:], in0=sc_ap, in1=dT[ln][:], op=ALU.mult)
                    yield pg.op("pe", "matmul", [L("PT"), ("TB", pb, 2)] + vkeys, [k0], b0[:, 0:dv], lhsT=PT[ln][:], rhs=vtokb, start=True, stop=True)
                    yield pg.op("pe", "matmul", [kQK, sbkey], [k1], b1[:, 0:dv], lhsT=qTb, rhs=Sb, start=True, stop=True)
                    yield pg.op("act", "copy", [k0], [L("oa")], out=oa[ln][:, 0:dv], in_=b0[:, 0:dv])
                    yield pg.op("dve", "scalar_tensor_tensor", [k1, L("oa")] + kdecs, [("otok", pb, hc)], out=ot[:, ocol:ocol + dv], in0=b1[:, 0:dv],
                                scalar=eG[:, hc:hc + 1], in1=oa[ln][:, 0:dv], op0=ALU.mult, op1=ALU.add)
                    yield pg.op("dve", "tensor_scalar_mul", vkeys + kdecs, [L("vw")], out=vw[ln][:, 0:dv], in0=vtok, scalar1=wj[:, hc:hc + 1])
                    yield pg.op("pe", "matmul", [L("vw"), ("TB", pb, 0), ("TB", pb, 1), ("TB", pb, 3)], [k0], b0[:, 0:dv], lhsT=ktokb, rhs=vw[ln][:, 0:dv],
                                start=True, stop=True)
                    yield pg.op("dve", "scalar_tensor_tensor", [skey, k0] + kdecs, [skey], out=S, in0=S, scalar=cc[:, hc:hc + 1],
                                in1=b0[:, 0:dv], op0=ALU.mult, op1=ALU.add)
                    yield pg.op("act", "copy", [skey], [sbkey], out=Sb, in_=S)

                tasks = []
                for h in range(4):
                    tasks.append(lambda ln, h=h: head(ln, h, FA[:, h, :], FA[:, 4 + h, :], TK[:, h, :], TK[:, 4 + h, :],
                                                      [("TK", pb, h), ("TK", pb, 4 + h)], 128, Sg[:, h, :], ("Sg", h), h * 128,
                                                      None, [kFA], QK[:, h, :], QK[:, 4 + h, :], TB[:, h, :], None, Sgb[:, h, :], ("Sgb", h),
                                                      gdn_h=h))
                for h in range(4):
                    tasks.append(lambda ln, h=h: head(ln, 4 + h, FR[:, h, :], FR[:, 4 + h, :], TK[:, 8 + h, :], TK[:, 12 + h, :],
                                                      [("TK", pb, 8 + h), ("TK", pb, 12 + h)], 128, Sr[:, h, :], ("Sr", h),
                                                      512 + h * 128, None, [kFR], QK[:, 8 + h, :], QK[:, 12 + h, :], TB[:, 4 + h, :],
                                                      TB[:, 8 + h, :], Srb[:, h, :], ("Srb", h)))
                def ssd_group(ln, g, ot=ot, pb=pb, Gi=Gi, eG=eG, wj=wj, cc=cc, sm=sm, kdecs=kdecs, TK=TK, QK=QK, TB=TB, scG=scG, kQK=kQK):
                    bank, bkey = PB[ln], ("pb", ln)
                    h0 = 8 * g
                    hc0 = 8 + h0
                    G = lambda *n: tuple(n) + ("g", g)
                    xk = [("TK", pb, 16 + 4 * g + j) for j in range(4)]
                    x8 = TK[:, 16 + 4 * g:20 + 4 * g, :].rearrange("p c (two d) -> p (c two) d", two=2)
                    dl = sm[:, 16 + di * 16 + h0:16 + di * 16 + h0 + 8].unsqueeze(2).broadcast_to([128, 8, 64])
                    yield pg.op("dve", "tensor_tensor", xk + kdecs, [G("xd")], out=g_xd[g][:], in0=x8, in1=dl, op=ALU.mult)
                    yield pg.op("dve", "tensor_tensor", ["ident"] + kdecs, [G("dg")], out=g_dg[g][:],
                                in0=ident[:].unsqueeze(1).broadcast_to([128, 8, 128]),
                                in1=Gi[:, hc0:hc0 + 8].unsqueeze(2).broadcast_to([128, 8, 128]), op=ALU.mult)
                    for hf in range(2):
                        b3 = bank[:, 0:512].rearrange("p (h t) -> p h t", h=4)
                        gi4 = Gi[:, hc0 + 4 * hf:hc0 + 4 * hf + 4].unsqueeze(2).broadcast_to([128, 4, 128])
                        yield pg.op("pe", "matmul", ["ones", G("dg")], [bkey], bank[:, 0:512], lhsT=ones[:],
                                    rhs=g_dg[g][:, 4 * hf:4 * hf + 4, :].rearrange("p h t -> p (h t)"), start=True, stop=True)
                        yield pg.op("dve", "tensor_tensor", [bkey] + kdecs, [G("dT")], out=g_dT[g][:], in0=b3, in1=gi4, op=ALU.subtract)
                        yield pg.op("dve", "tensor_tensor", [G("dT"), "msk"], [G("dT")], out=g_dT[g][:], in0=g_dT[g][:],
                                    in1=mD.unsqueeze(1).broadcast_to([128, 4, 128]), op=ALU.add)
                        yield pg.op("act", "activation", [G("dT")], [G("dT")], out=g_dT[g][:], in_=g_dT[g][:], func=AF.Exp)
                        yield pg.op("dve", "tensor_tensor", [G("dT"), ("scG", pb, g)], [G("PT", hf)], out=g_PT[g][:, 4 * hf:4 * hf + 4, :],
                                    in0=g_dT[g][:], in1=scG[:, g, :].unsqueeze(1).broadcast_to([128, 4, 128]), op=ALU.mult)
                    for j in range(8):
                        yield pg.op("pe", "matmul", [G("PT", j // 4), G("xd")], [bkey], bank[:, j * 64:(j + 1) * 64], lhsT=g_PT[g][:, j, :],
                                    rhs=g_xd[g][:, j, :], start=True, stop=True)
                    b8 = bank[:, 0:512].rearrange("p (h d) -> p h d", h=8)
                    skeys = [("Ss", h0 + j) for j in range(8)]
                    sbkeys = [("Ssb", h0 + j) for j in range(8)]
                    yield pg.op("act", "copy", [bkey], [G("oa")], out=g_oa[g][:], in_=b8)
                    yield pg.op("pe", "matmul", [kQK] + sbkeys, [bkey], bank[:, 0:512], lhsT=QK[:, 18 + g, :],
                                rhs=Ssb[:, h0:h0 + 8, :].rearrange("p h d -> p (h d)"), start=True, stop=True)
                    yield pg.op("dve", "tensor_tensor", [bkey] + kdecs, [G("tm")], out=g_tm[g][:], in0=b8,
                                in1=eG[:, hc0:hc0 + 8].unsqueeze(2).broadcast_to([128, 8, 64]), op=ALU.mult)
                    yield pg.op("dve", "tensor_tensor", [G("tm"), G("oa")], [("otok", pb, hc0 + j) for j in range(8)],
                                out=ot[:, 1024 + h0 * 64:1024 + (h0 + 8) * 64].rearrange("p (h d) -> p h d", h=8), in0=g_tm[g][:], in1=g_oa[g][:],
                                op=ALU.add)
                    yield pg.op("dve", "tensor_tensor", [G("xd")] + kdecs, [G("vw")], out=g_vw[g][:], in0=g_xd[g][:],
                                in1=wj[:, hc0:hc0 + 8].unsqueeze(2).broadcast_to([128, 8, 64]), op=ALU.mult)
                    yield pg.op("pe", "matmul", [G("vw"), ("TB", pb, 3)], [bkey], bank[:, 0:512], lhsT=TB[:, 12 + g, :],
                                rhs=g_vw[g][:].rearrange("p h d -> p (h d)"), start=True, stop=True)
                    yield pg.op("dve", "tensor_tensor", skeys + kdecs, skeys, out=Ss[:, h0:h0 + 8, :], in0=Ss[:, h0:h0 + 8, :],
                                in1=cc[:, hc0:hc0 + 8].unsqueeze(2).broadcast_to([128, 8, 64]), op=ALU.mult)
                    yield pg.op("dve", "tensor_tensor", skeys + [bkey], skeys, out=Ss[:, h0:h0 + 8, :], in0=Ss[:, h0:h0 + 8, :], in1=b8, op=ALU.add)
                    yield pg.op("act", "copy", skeys, sbkeys, out=Ssb[:, h0:h0 + 8, :], in_=Ss[:, h0:h0 + 8, :])

                ssd_tasks = [lambda ln, g=g: ssd_group(ln, g) for g in range(2)]
                tasks = tasks[0:4] + ssd_tasks + tasks[4:]
                if oi + 1 < len(order):
                    tasks.insert(6, lambda ln, nch_=order[oi + 1], pb_=1 - pb: preamble(ln, nch_, pb_, di))
                run_lanes(tasks, NL)
                pg.dma("sp", k.OF[di, ts, :], ot[:], semkey=("ot_st", pb), reads=[("otok", pb, hc_) for hc_ in range(24)])
        pg.barrier()


def stage_finish(k, li):
    pg, nc = k.pg, k.nc
    C, NT = k.C, k.NT
    nch, nct = NT // 128, C // 128
    last = li == DEPTH - 1
    pfx = "m%d_" % li
    AXX = mybir.AxisListType.X
    with ExitStack() as st:
        def sb(name, shape, dt=F32):
            return st.enter_context(nc.sbuf_tensor(pfx + name, shape, dt))

        def psum(name, shape):
            return st.enter_context(nc.psum_tensor(pfx + name, shape, F32))
        ident = sb("ident", [128, 128])
        Wa = sb("Wa", [128, 4, D], BF16)
        Wb = sb("Wb", [128, 4, D], BF16)
        Wc = sb("Wc", [128, 8, D], BF16)
        Wo = sb("Wo", [128, 8, D], BF16)
        NG = sb("NG", [128, 2048])
        Dx = sb("Dx", [128, 16])
        G5 = sb("G5", [128, D])
        LNG = sb("LNG", [128, D])
        LNB = sb("LNB", [128, D])
        of2 = [sb("of%d" % i, [128, 2048]) for i in range(2)]
        ob2 = [sb("ob%d" % i, [128, 2048]) for i in range(2)]
        FZ2 = [sb("FZ%d" % i, [128, 16, 128]) for i in range(2)]
        FX2 = [sb("FX%d" % i, [128, 8, 128]) for i in range(2)]
        FB2 = [sb("FB%d" % i, [128, 24, 128]) for i in range(2)]
        ZG = sb("ZG", [128, 2048])
        XS = sb("XS", [128, 1024])
        BG = sb("BG", [128, 3072])
        sq = sb("sq", [128, 2048])
        st1 = sb("st1", [128, 16])
        st2 = sb("st2", [128, 16])
        YT = sb("YT", [128, 16, 128], BF16)
        mg = sb("mg", [128, D])
        MT = sb("MT", [128, 8, 128], BF16)
        hb2 = [sb("hb%d" % i, [128, D]) for i in range(2)]
        tmp = sb("tmp", [128, D])
        lnw = ln_scratch(sb)
        psT = psum("psT", [128, 1024])
        psB = psum("psB", [128, 1024])
        psO = psum("psO", [128, 1024])

        pg.dma("sp", ident[:], k.ident[:, :], semkey="ident", writes=["ident"])
        pg.dma("pool", Wa[:], k.w_br_a[li].rearrange("(c p) n -> p c n", p=128), semkey="Wa", writes=["Wa"])
        pg.dma("pool", Wb[:], k.w_br_b[li].rearrange("(c p) n -> p c n", p=128), semkey="Wb", writes=["Wb"])
        pg.dma("pool", Wc[:], k.w_br_c[li].rearrange("(c p) n -> p c n", p=128), semkey="Wc", writes=["Wc"])
        pg.dma("pool", Wo[:], k.mix_w_out[li].rearrange("(c p) n -> p c n", p=128), semkey="Wo", writes=["Wo"])
        for h in range(4):
            pg.dma("sp", NG[:, h * 128:(h + 1) * 128], k.gdn_norm_g[li:li + 1, :].partition_broadcast(128), semkey="NG", writes=["NG"])
        pg.dma("sp", NG[:, 512:1024], k.ret_norm_g[li:li + 1, :].partition_broadcast(128), semkey="NG", writes=["NG"])
        pg.dma("sp", NG[:, 1024:2048], k.ssd_norm_g[li:li + 1, :].partition_broadcast(128), semkey="NG", writes=["NG"])
        pg.dma("sp", Dx[:], k.ssd_d[li:li + 1, :].partition_broadcast(128), semkey="Dx", writes=["Dx"])
        pg.dma("sp", LNG[:], k.ln_g[li, 1:2, :].partition_broadcast(128), semkey="LNG", writes=["LNG"])
        pg.dma("sp", LNB[:], k.ln_b[li, 1:2, :].partition_broadcast(128), semkey="LNB", writes=["LNB"])

        def TT(dst, src, n, rk, wk, func=None, dt_out=None):
            for c0 in range(0, n, 8):
                m = min(8, n - c0)
                for c in range(m):
                    pg.op("pe", "transpose", rk + ["ident"], [("psT", c // 4)], out=psT[:, c * 128:(c + 1) * 128],
                          in_=src(c0 + c), identity=ident[:])
                for half in range((m + 3) // 4):
                    w = min(4, m - half * 4) * 128
                    if func is None:
                        pg.op("act", "copy", [("psT", half)], wk, out=dst(c0 + half * 4, w), in_=psT[:, half * 512:half * 512 + w])
                    else:
                        pg.op("act", "activation", [("psT", half)], wk, out=dst(c0 + half * 4, w),
                              in_=psT[:, half * 512:half * 512 + w], func=func)

        cur = None
        ntile = 0
        for t in range(nch):
            is_ctx = t < nct
            if is_ctx and last:
                continue
            s = 1 if is_ctx else 0
            ts = slice(t * 128, (t + 1) * 128)
            par = ntile % 2
            ntile += 1
            of, ob, FZ, FX, FB, hb = of2[par], ob2[par], FZ2[par], FX2[par], FB2[par], hb2[par]
            kof, kob, kFZ, kFX, kFB, khb = ("of", par), ("ob", par), ("FZ", par), ("FX", par), ("FB", par), ("hb", par)
            if cur != s:
                pg.dma("sp", G5[:], k.modD[li, s:s + 1, 5 * 1024:6 * 1024].partition_broadcast(128), semkey="G5", writes=["G5"])
                cur = s
            pg.dma("sp", of[:], k.OF[0, ts, :], semkey=kof, writes=[kof])
            pg.dma("sp", ob[:], k.OF[1, ts, :], semkey=kob, writes=[kob])
            pg.dma("sp", FZ[:, 0:4, :], k.PF[G_Z:G_Z + 512, ts].rearrange("(c p) t -> p c t", p=128), semkey=kFZ, writes=[kFZ])
            pg.dma("sp", FZ[:, 4:8, :], k.PF[R_G:R_G + 512, ts].rearrange("(c p) t -> p c t", p=128), semkey=kFZ, writes=[kFZ])
            pg.dma("sp", FZ[:, 8:16, :], k.PF[S_Z:S_Z + 1024, ts].rearrange("(c p) t -> p c t", p=128), semkey=kFZ, writes=[kFZ])
            pg.dma("sp", FX[:], k.PF[S_XBC:S_XBC + 1024, ts].rearrange("(c p) t -> p c t", p=128), semkey=kFX, writes=[kFX])
            pg.dma("sp", FB[:], k.PF[BR_G:BR_G + 3072, ts].rearrange("(c p) t -> p c t", p=128), semkey=kFB, writes=[kFB])
            pg.dma("sp", hb[:], k.hD[ts, :], semkey=khb, writes=[khb])
            TT(lambda c, w: ZG[:, c * 128:c * 128 + w], lambda c: FZ[:, c, :], 16, [kFZ], ["ZG"], func=AF.Silu)
            TT(lambda c, w: XS[:, c * 128:c * 128 + w], lambda c: FX[:, c, :], 8, [kFX], ["XS"])
            TT(lambda c, w: BG[:, c * 128:c * 128 + w], lambda c: FB[:, c, :], 24, [kFB], ["BG"])
            o = of
            pg.op("dve", "tensor_tensor", [kof, kob], [kof], out=o[:], in0=of[:], in1=ob[:], op=ALU.add)
            pg.op("dve", "tensor_tensor", ["XS", "Dx"], ["XS"], out=XS[:].rearrange("p (h d) -> p h d", h=16),
                  in0=XS[:].rearrange("p (h d) -> p h d", h=16), in1=Dx[:, 0:16].unsqueeze(2).broadcast_to([128, 16, 64]), op=ALU.mult)
            pg.op("dve", "tensor_tensor", [kof, "XS"], [kof], out=o[:, 1024:2048], in0=o[:, 1024:2048], in1=XS[:], op=ALU.add)
            pg.op("dve", "tensor_tensor", [kof, "ZG"], [kof], out=o[:, 1024:2048], in0=o[:, 1024:2048], in1=ZG[:, 1024:2048], op=ALU.mult)
            pg.op("dve", "tensor_tensor", [kof], ["sq"], out=sq[:], in0=o[:], in1=o[:], op=ALU.mult)
            pg.op("dve", "tensor_reduce", ["sq"], ["st2"], out=st2[:, 0:8], in_=sq[:, 0:1024].rearrange("p (h d) -> p h d", h=8), axis=AXX, op=ALU.add)
            pg.op("dve", "tensor_reduce", ["sq"], ["st2"], out=st2[:, 8:10], in_=sq[:, 1024:2048].rearrange("p (h d) -> p h d", h=2), axis=AXX, op=ALU.add)
            pg.op("dve", "tensor_reduce", [kof], ["st1"], out=st1[:, 0:4], in_=o[:, 512:1024].rearrange("p (h d) -> p h d", h=4), axis=AXX, op=ALU.add)
            pg.op("act", "activation", ["st2"], ["st2"], out=st2[:, 0:4], in_=st2[:, 0:4], func=AF.Sqrt, scale=1.0 / 128, bias=1e-6)
            pg.op("act", "activation", ["st2"], ["st2"], out=st2[:, 8:10], in_=st2[:, 8:10], func=AF.Sqrt, scale=1.0 / 512, bias=1e-6)
            pg.op("dve", "tensor_scalar_mul", ["st1"], ["st1"], out=st1[:, 0:4], in0=st1[:, 0:4], scalar1=1.0 / 128)
            pg.op("dve", "tensor_tensor", ["st1"], ["st1"], out=st1[:, 4:8], in0=st1[:, 0:4], in1=st1[:, 0:4], op=ALU.mult)
            pg.op("dve", "scalar_tensor_tensor", ["st2", "st1"], ["st2"], out=st2[:, 4:8], in0=st2[:, 4:8], scalar=1.0 / 128,
                  in1=st1[:, 4:8], op0=ALU.mult, op1=ALU.subtract)
            pg.op("act", "activation", ["st2"], ["st2"], out=st2[:, 4:8], in_=st2[:, 4:8], func=AF.Sqrt, bias=1e-5)
            pg.op("dve", "reciprocal", ["st2"], ["st2"], out=st2[:, 0:10], in_=st2[:, 0:10])
            pg.op("dve", "tensor_tensor", [kof, "st1"], [kof], out=o[:, 512:1024].rearrange("p (h d) -> p h d", h=4),
                  in0=o[:, 512:1024].rearrange("p (h d) -> p h d", h=4), in1=st1[:, 0:4].unsqueeze(2).broadcast_to([128, 4, 128]), op=ALU.subtract)
            pg.op("dve", "tensor_tensor", [kof, "st2"], [kof], out=o[:, 0:1024].rearrange("p (h d) -> p h d", h=8),
                  in0=o[:, 0:1024].rearrange("p (h d) -> p h d", h=8), in1=st2[:, 0:8].unsqueeze(2).broadcast_to([128, 8, 128]), op=ALU.mult)
            pg.op("dve", "tensor_tensor", [kof, "st2"], [kof], out=o[:, 1024:2048].rearrange("p (h d) -> p h d", h=2),
                  in0=o[:, 1024:2048].rearrange("p (h d) -> p h d", h=2), in1=st2[:, 8:10].unsqueeze(2).broadcast_to([128, 2, 512]), op=ALU.mult)
            pg.op("dve", "tensor_tensor", [kof, "NG"], [kof], out=o[:], in0=o[:], in1=NG[:], op=ALU.mult)
            pg.op("dve", "tensor_tensor", [kof, "ZG"], [kof], out=o[:, 0:1024], in0=o[:, 0:1024], in1=ZG[:, 0:1024], op=ALU.mult)
            TT(lambda c, w: YT[:, c:c + w // 128, :], lambda c: o[:, c * 128:(c + 1) * 128], 16, [kof], ["YT"])
            for bi, (W, c0, ncn, wk) in enumerate(((Wa, 0, 4, "Wa"), (Wb, 4, 4, "Wb"), (Wc, 8, 8, "Wc"))):
                for half in range(2):
                    hs = slice(half * 512, (half + 1) * 512)
                    for c in range(ncn):
                        pg.op("pe", "matmul", ["YT", wk], [("psB", half)], psB[:, hs], lhsT=YT[:, c0 + c, :], rhs=W[:, c, hs],
                              start=(c == 0), stop=(c == ncn - 1))
                    gs = slice(bi * 1024 + half * 512, bi * 1024 + (half + 1) * 512)
                    if bi == 0:
                        pg.op("dve", "tensor_tensor", [("psB", half), "BG"], [("mg", half)], out=mg[:, hs], in0=psB[:, hs], in1=BG[:, gs], op=ALU.mult)
                    else:
                        pg.op("dve", "tensor_tensor", [("psB", half), "BG"], [("tmp", half)], out=tmp[:, hs], in0=psB[:, hs], in1=BG[:, gs], op=ALU.mult)
                        pg.op("dve", "tensor_tensor", [("mg", half), ("tmp", half)], [("mg", half)], out=mg[:, hs], in0=mg[:, hs], in1=tmp[:, hs], op=ALU.add)
            TT(lambda c, w: MT[:, c:c + w // 128, :], lambda c: mg[:, c * 128:(c + 1) * 128], 8, [("mg", 0), ("mg", 1)], ["MT"])
            for half in range(2):
                hs = slice(half * 512, (half + 1) * 512)
                for c in range(8):
                    pg.op("pe", "matmul", ["MT", "Wo"], [("psO", half)], psO[:, hs], lhsT=MT[:, c, :], rhs=Wo[:, c, hs],
                          start=(c == 0), stop=(c == 7))
                pg.op("dve", "tensor_tensor", [("psO", half), "G5"], [("tmp", half)], out=tmp[:, hs], in0=psO[:, hs], in1=G5[:, hs], op=ALU.mult)
            pg.op("dve", "scalar_tensor_tensor", [khb, ("tmp", 0), ("tmp", 1)], [khb], out=hb[:], in0=hb[:], scalar=ALPHA, in1=tmp[:],
                  op0=ALU.mult, op1=ALU.add)
            layer_norm_tile(k, hb, khb, lnw, LNG, LNB)
            pg.dma("pool", k.hD[ts, :], hb[:], semkey=("hb_st", par), reads=[khb])
        pg.barrier()


def stage_proj(k, li):
    pg, nc = k.pg, k.nc
    C, NT, L = k.C, k.NT, k.L
    pfx = "p%d_" % li
    with ExitStack() as st:
        def sb(name, shape, dt):
            return st.enter_context(nc.sbuf_tensor(pfx + name, shape, dt))

        def psum(name, shape):
            return st.enter_context(nc.psum_tensor(pfx + name, shape, F32))
        ident = sb("ident", [128, 128], F32)
        ones = sb("ones", [128, 128], F32)
        rott = sb("rott", [128, 128], F32)
        sc = sb("sc", [128, 2, 8], F32)
        sh = sb("sh", [128, 2, 8], F32)
        hb = [sb("hb%d" % i, [128, D], F32) for i in range(2)]
        uT = sb("uT", [128, 8, NT], BF16)
        Wsup = [[sb("Wsup%d_%d" % (i, j), [128, 8, 512], BF16) for j in range(2)] for i in range(2)]
        sup_cur = [(-1, 1), (-1, 1)]
        SEGS = [(0, 1536), (1536, 512), (2048, 8), (2056, 8), (2064, 1024), (3088, 1024), (4112, 1024), (5136, 1536), (6672, 32), (6704, 3072)]
        pf = [sb("pf%d" % i, [128, NT], F32) for i in range(3)]
        cv = sb("cv", [128, NT], F32)
        t1 = sb("t1", [128, 512], F32)
        t2 = sb("t2", [128, 512], F32)
        cosb = sb("cosb", [128, 512], F32)
        sinb = sb("sinb", [128, 512], F32)
        cwg = sb("cwg", [128, 12, 5], F32)
        cws = sb("cws", [128, 12, 5], F32)
        cbs = sb("cbs", [128, 12], F32)
        prm = sb("prm", [32, 6], F32)
        psT = psum("psT", [128, 1024])
        psP = [psum("psP%d" % i, [128, 512]) for i in range(2)]
        psR = psum("psR", [128, 512])
        psC = [psum("psC%d" % i, [128, 512]) for i in range(2)]
        xb = [sb("xb%d" % i, [128, NT + 6], BF16) for i in range(2)]
        dgw = [sb("dgw%d" % i, [128, 5, 128], BF16) for i in range(2)]
        for i in range(2):
            pg.op("dve", "memset", [], [("xb", i)], xb[i][:], 0.0)

        pg.dma("sp", ident[:], k.ident[:, :], semkey="ident", writes=["ident"])
        pg.dma("sp", rott[:], k.rott[:, :], semkey="rott", writes=["rott"])
        pg.op("dve", "memset", [], ["ones"], ones[:], 1.0)
        for kk in range(5):
            pg.dma("sp", cwg[:, :, kk], k.gdn_conv_w[li, kk].rearrange("(c p) -> p c", p=128), semkey="cwg",
                   writes=["cwg"], nc_ok=True)
            pg.dma("sp", cws[:, :, kk], k.ssd_conv_w[li, kk].rearrange("(c p) -> p c", p=128), semkey="cws",
                   writes=["cws"], nc_ok=True)
        pg.dma("sp", cbs[:], k.ssd_conv_b[li].rearrange("(c p) -> p c", p=128), semkey="cbs", writes=["cbs"], nc_ok=True)
        pg.op("dve", "memset", [], ["prm"], prm[:], 0.0)
        pg.dma("sp", prm[0:8, 0:1], k.gdn_dt_bias[li].rearrange("a (b o) -> (a b) o", o=1), semkey="prm", writes=["prm"], nc_ok=True)
        pg.dma("sp", prm[0:8, 1:2], k.gdn_a_log[li].rearrange("a (b o) -> (a b) o", o=1), semkey="prm", writes=["prm"], nc_ok=True)
        pg.dma("sp", prm[0:32, 2:3], k.ssd_dt_bias[li].rearrange("a (b o) -> (a b) o", o=1), semkey="prm", writes=["prm"], nc_ok=True)
        pg.dma("sp", prm[0:32, 3:4], k.ssd_a_log[li].rearrange("a (b o) -> (a b) o", o=1), semkey="prm", writes=["prm"], nc_ok=True)
        for c in (1, 3):
            pg.op("act", "activation", ["prm"], ["prm"], out=prm[:, c:c + 1], in_=prm[:, c:c + 1], func=AF.Exp)
            pg.op("dve", "tensor_scalar_mul", ["prm"], ["prm"], out=prm[:, c:c + 1], in0=prm[:, c:c + 1], scalar1=-1.0)
        for s in range(2):
            pg.dma("sp", sc[:, s, :], k.modD[li, s, 4 * 1024:5 * 1024].rearrange("(k p) -> p k", p=128),
                   semkey="sc", writes=["sc"], nc_ok=True)
            pg.dma("sp", sh[:, s, :], k.modD[li, s, 3 * 1024:4 * 1024].rearrange("(k p) -> p k", p=128),
                   semkey="sh", writes=["sh"], nc_ok=True)
        pg.op("dve", "tensor_scalar_add", ["sc"], ["sc"], out=sc[:], in0=sc[:], scalar1=1.0)

        for t in range(NT // 128):
            b = t % 2
            s = 1 if t < C // 128 else 0
            pg.dma("sp", hb[b][:], k.hD[t * 128:(t + 1) * 128, :], semkey=("hb", b), writes=[("hb", b)])
            for kc in range(8):
                pg.op("pe", "transpose", [("hb", b), "ident"], [("psT", kc // 4)],
                      out=psT[:, kc * 128:(kc + 1) * 128], in_=hb[b][:, kc * 128:(kc + 1) * 128], identity=ident[:])
            for kc in range(8):
                pg.op("act", "activation", [("psT", kc // 4), "sc", "sh"], [("uT", t)],
                      out=uT[:, kc, t * 128:(t + 1) * 128], in_=psT[:, kc * 128:(kc + 1) * 128],
                      func=AF.Identity, bias=sh[:, s, kc:kc + 1], scale=sc[:, s, kc:kc + 1])
        uT_keys = [("uT", t) for t in range(NT // 128)]

        chunks = []
        for c in range(12):
            kind = "l2q" if c < 4 else ("l2k" if c < 8 else "conv")
            chunks.append((G_QKV + c * 128, 128, kind, (cwg, c, None)))
        for c in range(4):
            chunks.append((G_Z + c * 128, 128, "raw", None))
        chunks.append((G_B, 8, "sig", None))
        chunks.append((G_A, 8, "gdn_g", None))
        for c in range(4):
            chunks.append((R_Q + c * 128, 128, "rope", 1.0))
        for c in range(4):
            chunks.append((R_K + c * 128, 128, "rope", 128.0 ** -0.5))
        for c in range(8):
            chunks.append((R_V + c * 128, 128, "raw", None))
        for c in range(8):
            chunks.append((S_Z + c * 128, 128, "raw", None))
        for c in range(12):
            chunks.append((S_XBC + c * 128, 128, "conv", (cws, c, cbs)))
        chunks.append((S_DT, 32, "ssd_dt", None))
        for c in range(24):
            chunks.append((BR_G + c * 128, 128, "sig", None))

        heavy = [c for c in chunks if c[2] in ("conv", "l2q", "l2k", "rope")]
        light = [c for c in chunks if c[2] not in ("conv", "l2q", "l2k", "rope")]
        chunks = []
        while heavy or light:
            if heavy:
                chunks.append(heavy.pop(0))
            if light:
                chunks.append(light.pop(0))
        assert C <= 512
        blocks = [(0, C)] + [(t0, min(512, NT - t0)) for t0 in range(C, NT, 512)]
        nconv = 0
        for ci, (c0, w, kind, ex) in enumerate(chunks):
            b = ci % 3
            P, pk = pf[b], ("pf", b)
            xbuf, xbk = xb[nconv % 2], ("xb", nconv % 2)
            stream = 0 if kind in ("conv", "l2q", "l2k", "rope") else 1
            sa_, sw_ = [sg for sg in SEGS if sg[0] <= c0 < sg[0] + sg[1]][0]
            S0 = sa_ + ((c0 - sa_) // 512) * 512
            SW = min(512, sa_ + sw_ - S0)
            if sup_cur[stream][0] != S0:
                j = (sup_cur[stream][1] + 1) % 2
                sup_cur[stream] = (S0, j)
                pg.dma("pool", Wsup[stream][j][:, :, 0:SW], k.mix_w_in[li, :, S0:S0 + SW].rearrange("(k p) n -> p k n", p=128),
                       semkey=("Wsup", stream, j), writes=[("Wsup", stream, j)])
            j = sup_cur[stream][1]
            wk = ("Wsup", stream, j)
            Wv = Wsup[stream][j]
            wo = c0 - S0
            for bi, (t0, n) in enumerate(blocks):
                i = bi % 2
                for kc in range(8):
                    pg.op("pe", "matmul", [wk] + uT_keys, [("psP", i)],
                          psP[i][0:w, 0:n], lhsT=Wv[:, kc, wo:wo + w], rhs=uT[:, kc, t0:t0 + n],
                          start=(kc == 0), stop=(kc == 7))
                if kind in ("conv", "l2q", "l2k"):
                    off = 2 if t0 < C else 4
                    pg.op("act", "copy", [("psP", i)], [xbk], out=xbuf[:, off + t0:off + t0 + n], in_=psP[i][:, 0:n])
                else:
                    pg.op("act", "copy", [("psP", i)], [pk], out=P[0:w, t0:t0 + n], in_=psP[i][0:w, 0:n])
            if kind in ("conv", "l2q", "l2k"):
                cw, cc, cb = ex
                dw = dgw[nconv % 2]
                dwk = ("dgw", nconv % 2)
                for kk in range(5):
                    pg.op("dve", "tensor_scalar_mul", ["ident", "cwg", "cws"], [dwk], out=dw[:, kk, :], in0=ident[:],
                          scalar1=cw[:, cc, kk:kk + 1])
                for bi, (t0, n) in enumerate(blocks):
                    off = 2 if t0 < C else 4
                    i = bi % 2
                    for kk in range(5):
                        pg.op("pe", "matmul", [dwk, xbk], [("psC", i)], psC[i][:, 0:n], lhsT=dw[:, kk, :],
                              rhs=xbuf[:, off + t0 + kk - 2:off + t0 + kk - 2 + n], start=(kk == 0), stop=(kk == 4))
                    if cb is None:
                        pg.op("act", "activation", [("psC", i)], [pk], out=P[:, t0:t0 + n], in_=psC[i][:, 0:n], func=AF.Silu)
                    else:
                        pg.op("act", "activation", [("psC", i), "cbs"], [pk], out=P[:, t0:t0 + n], in_=psC[i][:, 0:n], func=AF.Silu,
                              bias=cb[:, cc:cc + 1])
                nconv += 1
                if kind != "conv":
                    scl = 128.0 ** -0.5 if kind == "l2q" else 1.0
                    pg.op("act", "activation", [pk], ["cv"], out=cv[:], in_=P[:], func=AF.Square)
                    for (t0, n) in blocks:
                        pg.op("pe", "matmul", ["ones", "cv"], ["psR"], psR[:, 0:n], lhsT=ones[:], rhs=cv[:, t0:t0 + n],
                              start=True, stop=True)
                        pg.op("act", "activation", ["psR"], ["t1"], out=t1[:, 0:n], in_=psR[:, 0:n], func=AF.Ln, bias=1e-6)
                        pg.op("act", "activation", ["t1"], ["t1"], out=t1[:, 0:n], in_=t1[:, 0:n], func=AF.Exp, scale=-0.5)
                        pg.op("dve", "scalar_tensor_tensor", [pk, "t1"], [pk], out=P[:, t0:t0 + n],
                              in0=P[:, t0:t0 + n], scalar=scl, in1=t1[:, 0:n], op0=ALU.mult, op1=ALU.mult)
            elif kind == "rope":
                if ex != 1.0:
                    pg.op("dve", "tensor_scalar_mul", [pk], [pk], out=P[:], in0=P[:], scalar1=float(ex))
                for t0 in range(C, NT, 512):
                    n = min(512, NT - t0)
                    pg.dma("sp", cosb[:, 0:n], k.cosT[:, t0 - C:t0 - C + n], semkey="cosb", writes=["cosb"])
                    pg.dma("sp", sinb[:, 0:n], k.sinT[:, t0 - C:t0 - C + n], semkey="sinb", writes=["sinb"])
                    pg.op("pe", "matmul", ["rott", pk], ["psR"], psR[:, 0:n], lhsT=rott[:], rhs=P[:, t0:t0 + n],
                          start=True, stop=True)
                    pg.op("dve", "tensor_tensor", [pk, "cosb"], ["t1"], out=t1[:, 0:n], in0=P[:, t0:t0 + n],
                          in1=cosb[:, 0:n], op=ALU.mult)
                    pg.op("dve", "tensor_tensor", ["psR", "sinb"], ["t2"], out=t2[:, 0:n], in0=psR[:, 0:n],
                          in1=sinb[:, 0:n], op=ALU.mult)
                    pg.op("dve", "tensor_tensor", ["t1", "t2"], [pk], out=P[:, t0:t0 + n], in0=t1[:, 0:n],
                          in1=t2[:, 0:n], op=ALU.add)
            elif kind == "sig":
                pg.op("act", "activation", [pk], [pk], out=P[0:w, :], in_=P[0:w, :], func=AF.Sigmoid)
            elif kind in ("gdn_g", "ssd_dt"):
                col = 0 if kind == "gdn_g" else 2
                pg.op("act", "activation", [pk, "prm"], [pk], out=P[0:w, :], in_=P[0:w, :], func=AF.Exp,
                      bias=prm[0:w, col:col + 1])
                pg.op("act", "activation", [pk], [pk], out=P[0:w, :], in_=P[0:w, :], func=AF.Ln, bias=1.0)
                if kind == "gdn_g":
                    pg.op("dve", "tensor_scalar_mul", [pk, "prm"], [pk], out=P[0:w, :], in0=P[0:w, :],
                          scalar1=prm[0:w, 1:2])
                else:
                    pg.op("dve", "tensor_scalar_mul", [pk, "prm"], ["cv"], out=cv[0:w, :], in0=P[0:w, :],
                          scalar1=prm[0:w, 3:4])
                    pg.dma("sp", k.PF[X_LOGA:X_LOGA + w, :], cv[0:w, :], semkey="cv_st", reads=["cv"])
            pg.dma("sp", k.PF[c0:c0 + w, :], P[0:w, :], semkey=("pf_st", b), reads=[pk])
        pg.barrier()


def build_program(L, C, stop_after=None, dbg=()):
    nc = bass.Bass("TRN2", target_bir_lowering=False)
    k = K()
    k.nc = nc
    k.L, k.C, k.NT = L, C, L + C
    k.stop_scan = (stop_after is not None and stop_after.startswith("scan"))

    def din(name, shape):
        return nc.dram_tensor(name, list(shape), F32, kind="ExternalInput").ap()
    k.x = din("x", [L, D])
    k.ctx = din("ctx", [C, D])
    k.cvec = din("cvec", [2, D])
    k.ident = din("ident", [128, 128])
    k.rott = din("rott", [128, 128])
    k.cosT = din("cosT", [128, L])
    k.sinT = din("sinT", [128, L])
    k.masks = din("masks", [8, 128, 128])
    for name, shape in WEIGHT_SHAPES:
        setattr(k, name, din(name, shape))
    k.out = nc.dram_tensor("out", [L, D], F32, kind="ExternalOutput").ap()

    def scratch(name, shape):
        if name in dbg:
            return nc.dram_tensor(name, list(shape), F32, kind="ExternalOutput").ap()
        return nc.dram_tensor(name, list(shape), F32).ap()
    k.hD = scratch("hD", [k.NT, D])
    k.modD = scratch("modD", [DEPTH, 2, NADA * D])
    k.PF = scratch("PF", [NF, k.NT])
    k.OF = scratch("OF", [2, k.NT, 2048])
    k.dbgX = scratch("dbgX", [128, 1024]) if "dbgX" in dbg else None

    nct = C // 128

    def src_in(t):
        if t < nct:
            return k.ctx[t * 128:(t + 1) * 128, :]
        return k.x[(t - nct) * 128:(t - nct + 1) * 128, :]

    def src_h(t):
        return k.hD[t * 128:(t + 1) * 128, :]

    def dst_out(t):
        assert t >= nct
        return k.out[(t - nct) * 128:(t - nct + 1) * 128, :]

    with ExitStack() as stack:
        k.pg = Prog(nc, stack)

        def body():
            stage_mod(k)
            if stop_after == "mod":
                return
            for li in range(DEPTH):
                last = li == DEPTH - 1
                stage_ffn(k, li, 0, src_in if li == 0 else src_h, src_h, True)
                if stop_after == "ffn%d0" % li:
                    return
                stage_proj(k, li)
                if stop_after == "proj%d" % li:
                    return
                stage_mix(k, li)
                if stop_after in ("mix%d" % li, "scan%d" % li):
                    return
                stage_ffn(k, li, 1, src_h, dst_out if last else src_h, not last)
        body()
        k.pg.barrier()
        k.pg.emit_all()
    return nc


WEIGHT_SHAPES = [
    ("ada_w", [DEPTH, D, NADA * D]), ("ada_b", [DEPTH, NADA * D]), ("ln_g", [DEPTH, 3, D]), ("ln_b", [DEPTH, 3, D]),
    ("ffn_w13", [DEPTH, 2, D, 2 * DFF]), ("ffn_w2", [DEPTH, 2, DFF, D]), ("mix_w_in", [DEPTH, D, N_IN]),
    ("gdn_conv_w", [DEPTH, 5, 1536]), ("gdn_a_log", [DEPTH, 2, 4]), ("gdn_dt_bias", [DEPTH, 2, 4]),
    ("gdn_norm_g", [DEPTH, 128]), ("ret_decay", [DEPTH, 2, 4]), ("ret_norm_g", [DEPTH, 512]),
    ("ssd_conv_w", [DEPTH, 5, 1536]), ("ssd_conv_b", [DEPTH, 1536]), ("ssd_a_log", [DEPTH, 2, 16]),
    ("ssd_dt_bias", [DEPTH, 2, 16]), ("ssd_d", [DEPTH, 16]), ("ssd_norm_g", [DEPTH, 1024]),
    ("w_br_a", [DEPTH, 512, D]), ("w_br_b", [DEPTH, 512, D]), ("w_br_c", [DEPTH, D, D]), ("mix_w_out", [DEPTH, D, D]),
]


def host_consts(L):
    ident = np.eye(128, dtype=np.float32)
    rott = np.zeros((128, 128), np.float32)
    for m in range(64):
        rott[m + 64, m] = -1.0
        rott[m, m + 64] = 1.0
    t = np.arange(L)
    row_id = (t // 64).astype(np.float32)
    col_id = (t % 64).astype(np.float32)
    n_freq = 32
    inv_freq = (np.float32(10000.0) ** (-np.arange(n_freq, dtype=np.float32) / n_freq)).astype(np.float32)
    ang = np.concatenate([row_id[:, None] * inv_freq, col_id[:, None] * inv_freq], axis=-1)
    cosT = np.concatenate([np.cos(ang), np.cos(ang)], axis=1).T.astype(np.float32).copy()
    sinT = np.concatenate([np.sin(ang), np.sin(ang)], axis=1).T.astype(np.float32).copy()
    p = np.arange(128)
    BIG = -30000.0
    masks = np.zeros((8, 128, 128), np.float32)
    masks[6] = ((p[:, None] // 32) == (p[None, :] // 32)).astype(np.float32)
    masks[7] = 1.0 - masks[6]
    masks[4] = np.where(p[None, :] < p[:, None], 0.0, BIG)
    masks[5] = np.where(p[None, :] > p[:, None], 0.0, BIG)
    masks[0] = (p[:, None] <= p[None, :]).astype(np.float32)
    masks[1] = (p[:, None] >= p[None, :]).astype(np.float32)
    masks[2] = np.where(p[None, :] >= p[:, None], 0.0, BIG)
    masks[3] = np.where(p[None, :] <= p[:, None], 0.0, BIG)
    return {"ident": ident, "rott": rott, "cosT": cosT, "sinT": sinT, "masks": masks}


def make_in_map(b, inputs, consts):
    m = {"x": np.ascontiguousarray(inputs["x"][b]), "ctx": np.ascontiguousarray(inputs["ctx"][b]),
         "cvec": np.ascontiguousarray(np.stack([inputs["c"][b], inputs["c_ctx"]]))}
    m.update(consts)
    for name, _ in WEIGHT_SHAPES:
        m[name] = inputs[name]
    return m


def kernel(**inputs):
    inputs = {k_: np.asarray(v, dtype=np.float32) for k_, v in inputs.items()}
    B, L, _ = inputs["x"].shape
    C = inputs["ctx"].shape[1]
    nc = build_program(L, C)
    consts = host_consts(L)
    in_maps = [make_in_map(b, inputs, consts) for b in range(B)]
    res = run_bass_kernel_spmd(nc, in_maps, core_ids=list(range(B)))
    return np.stack([r["out"] for r in res.results], axis=0).astype(np.float32)
```
